# Optimizing a Trainium2 kernel written in Bass

```python
import jax, jax.numpy as jnp
from jax import lax
import numpy as np

D_MODEL = 1024
BATCH = 8
SEQ = 2048
DEPTH = 2
DEC_BATCH = 128
DEC_SEQ = 1
PAST_LEN = 16384
PAGE_SIZE = 128

E_A = D_MODEL
G_A = 4
HD_A = E_A // G_A
CHUNK = 128
E_B = D_MODEL
CONV_W = 3
D_FF = 2816
EPS = 1e-6
IN_COLS = 2 * E_A + 3 * E_B + 2 * D_MODEL
SPLITS = (E_A, 2 * E_A, 2 * E_A + E_B, 2 * E_A + 2 * E_B, 2 * E_A + 3 * E_B, 2 * E_A + 3 * E_B + D_MODEL)

kernel_name = "hybrid_chunkmlp_shortconv_macaron_step"


def rms_norm(x, g):
    xf = x.astype(jnp.float32)
    y = xf * lax.rsqrt(jnp.mean(xf * xf, axis=-1, keepdims=True) + EPS)
    return (y * g.astype(jnp.float32)).astype(x.dtype)


def layer_norm(x, g, b):
    xf = x.astype(jnp.float32)
    mu = jnp.mean(xf, axis=-1, keepdims=True)
    xc = xf - mu
    y = xc * lax.rsqrt(jnp.mean(xc * xc, axis=-1, keepdims=True) + EPS)
    return (y * g.astype(jnp.float32) + b.astype(jnp.float32)).astype(x.dtype)


def swiglu(x, w_gate, w_up, w_down):
    return (jax.nn.silu(x @ w_gate) * (x @ w_up)) @ w_down


def chunk_spatial_gate(u, v, w_s, b_s):
    bsz, L, _ = v.shape
    Lp = -(-L // CHUNK) * CHUNK
    vp = jnp.pad(v, ((0, 0), (0, Lp - L), (0, 0)))
    vc = vp.reshape(bsz, Lp // CHUNK, CHUNK, G_A, HD_A)
    causal = jnp.tril(jnp.ones((CHUNK, CHUNK), dtype=bool))
    w = jnp.where(causal[None], w_s, jnp.zeros((), w_s.dtype))
    z = jnp.einsum('gts,bcsgd->bctgd', w, vc) + jnp.transpose(b_s)[None, None, :, :, None]
    z = z.reshape(bsz, Lp, E_A)[:, :L]
    return u * z


def hybrid_layer(x, conv_buf, ffn1_norm, ffn1_w_gate, ffn1_w_up, ffn1_w_down, mix_norm, w_in, b_in,
                 v_ln_gain, v_ln_bias, w_spatial, b_spatial, conv_w, w_out,
                 ffn2_norm, ffn2_w_gate, ffn2_w_up, ffn2_w_down):
    h = x + 0.5 * swiglu(rms_norm(x, ffn1_norm), ffn1_w_gate, ffn1_w_up, ffn1_w_down)
    n = rms_norm(h, mix_norm)
    z = n @ w_in + b_in
    u, v, gate_b, gate_c, x_in, g_a, g_b = jnp.split(z, SPLITS, axis=-1)
    u = jax.nn.gelu(u, approximate=False)
    v = layer_norm(jax.nn.gelu(v, approximate=False), v_ln_gain, v_ln_bias)
    y_a = chunk_spatial_gate(u, v, w_spatial, b_spatial)
    L = x.shape[1]
    xg = gate_c * x_in
    xc = jnp.concatenate([conv_buf.astype(xg.dtype), xg], axis=1)
    conv = conv_w[0] * xc[:, 0:L]
    for k in range(1, CONV_W):
        conv = conv + conv_w[k] * xc[:, k:k + L]
    y_b = gate_b * conv
    new_buf = xc[:, -(CONV_W - 1):]
    m = jax.nn.sigmoid(g_a) * y_a + jax.nn.sigmoid(g_b) * y_b
    h = h + m @ w_out
    h = h + 0.5 * swiglu(rms_norm(h, ffn2_norm), ffn2_w_gate, ffn2_w_up, ffn2_w_down)
    return h, new_buf, v


def setup_inputs(seed: int = 0) -> dict:
    key = jax.random.key(seed)
    ks = jax.random.split(key, 24)
    f32 = jnp.float32
    nrm = lambda k, shape, s: jax.random.normal(k, shape, f32) * s
    d = D_MODEL
    return {
        "x_prompt": nrm(ks[0], (BATCH, SEQ, d), 1.0),
        "x_sample": nrm(ks[1], (DEC_BATCH, DEC_SEQ, d), 1.0),
        "state_conv": nrm(ks[2], (DEPTH, DEC_BATCH, CONV_W - 1, E_B), 0.5),
        "ffn1_norm": 1.0 + nrm(ks[3], (DEPTH, d), 0.02),
        "ffn1_w_gate": nrm(ks[4], (DEPTH, d, D_FF), d ** -0.5),
        "ffn1_w_up": nrm(ks[5], (DEPTH, d, D_FF), d ** -0.5),
        "ffn1_w_down": nrm(ks[6], (DEPTH, D_FF, d), D_FF ** -0.5),
        "mix_norm": 1.0 + nrm(ks[7], (DEPTH, d), 0.02),
        "w_in": nrm(ks[8], (DEPTH, d, IN_COLS), d ** -0.5),
        "b_in": nrm(ks[9], (DEPTH, IN_COLS), 0.02),
        "v_ln_gain": 1.0 + nrm(ks[10], (DEPTH, E_A), 0.02),
        "v_ln_bias": nrm(ks[11], (DEPTH, E_A), 0.02),
        "w_spatial": nrm(ks[12], (DEPTH, G_A, CHUNK, CHUNK), CHUNK ** -0.5),
        "b_spatial": 1.0 + nrm(ks[13], (DEPTH, G_A, CHUNK), 0.02),
        "conv_w": nrm(ks[14], (DEPTH, CONV_W, E_B), CONV_W ** -0.5),
        "w_out": nrm(ks[15], (DEPTH, d, d), d ** -0.5),
        "ffn2_norm": 1.0 + nrm(ks[16], (DEPTH, d), 0.02),
        "ffn2_w_gate": nrm(ks[17], (DEPTH, d, D_FF), d ** -0.5),
        "ffn2_w_up": nrm(ks[18], (DEPTH, d, D_FF), d ** -0.5),
        "ffn2_w_down": nrm(ks[19], (DEPTH, D_FF, d), D_FF ** -0.5),
        "final_norm": 1.0 + nrm(ks[20], (d,), 0.02),
    }


def reference(x_prompt, x_sample, state_conv, ffn1_norm, ffn1_w_gate, ffn1_w_up, ffn1_w_down,
              mix_norm, w_in, b_in, v_ln_gain, v_ln_bias, w_spatial, b_spatial, conv_w, w_out,
              ffn2_norm, ffn2_w_gate, ffn2_w_up, ffn2_w_down, final_norm):
    hp = x_prompt
    hs = x_sample
    conv_p_list, conv_s_list, v_s_list = [], [], []
    for l in range(DEPTH):
        params = (ffn1_norm[l], ffn1_w_gate[l], ffn1_w_up[l], ffn1_w_down[l], mix_norm[l], w_in[l], b_in[l],
                  v_ln_gain[l], v_ln_bias[l], w_spatial[l], b_spatial[l], conv_w[l], w_out[l],
                  ffn2_norm[l], ffn2_w_gate[l], ffn2_w_up[l], ffn2_w_down[l])
        zero_buf = jnp.zeros((hp.shape[0], CONV_W - 1, E_B), dtype=hp.dtype)
        hp, buf_p, _ = hybrid_layer(hp, zero_buf, *params)
        hs, buf_s, v_s = hybrid_layer(hs, state_conv[l], *params)
        conv_p_list.append(buf_p)
        conv_s_list.append(buf_s)
        v_s_list.append(v_s)
    y_prompt = rms_norm(hp, final_norm)
    y_sample = rms_norm(hs, final_norm)
    new_conv_prompt = jnp.stack(conv_p_list, axis=0)
    new_conv_sample = jnp.stack(conv_s_list, axis=0)
    new_chunk_v_sample = jnp.stack(v_s_list, axis=0)
    return (y_prompt, y_sample, new_conv_prompt, new_conv_sample, new_chunk_v_sample)
```

```python
import numpy as np
from contextlib import ExitStack
import concourse.bass as bass
import concourse.mybir as mybir
from concourse.bass_utils import run_bass_kernel_spmd

F32 = mybir.dt.float32
BF16 = mybir.dt.bfloat16
AF = mybir.ActivationFunctionType
ALU = mybir.AluOpType

D = 1024
DFF = 2816
NL = 2
NCORE = 8
SEQ = 2048
SB = 16
NTOK = SEQ + SB
INC = 7168
TT = [(0, 512), (512, 512), (1024, 512), (1536, 512), (2048, 16)]
EPS = 1e-6
NSLOT = 16
GROUPS = [(0, 6), (6, 14), (14, 22)]
ENGS = ("pe", "act", "dve", "pool", "sp")

W_SHAPES = [
    ("ffn1_norm", [NL, D]), ("ffn1_w_gate", [NL, D, DFF]), ("ffn1_w_up", [NL, D, DFF]),
    ("ffn1_w_down", [NL, DFF, D]), ("mix_norm", [NL, D]), ("w_in", [NL, D, INC]),
    ("b_in", [NL, INC]), ("v_ln_gain", [NL, D]), ("v_ln_bias", [NL, D]),
    ("w_spatial", [NL, 4, 128, 128]), ("b_spatial", [NL, 4, 128]), ("conv_w", [NL, 3, D]),
    ("w_out", [NL, D, D]), ("ffn2_norm", [NL, D]), ("ffn2_w_gate", [NL, D, DFF]),
    ("ffn2_w_up", [NL, D, DFF]), ("ffn2_w_down", [NL, DFF, D]), ("final_norm", [D]),
]


class Res:
    __slots__ = ("w", "r", "excl")

    def __init__(self, seed=None):
        self.w = None
        self.r = dict(seed) if seed else {}
        self.excl = False


class Prog:
    def __init__(self, nc, es):
        self.nc = nc
        self.es = es
        self.engs = {"pe": nc.tensor, "act": nc.scalar, "dve": nc.vector,
                     "pool": nc.gpsimd, "sp": nc.sync}
        self.sems = {}
        self.counts = {}
        self.cur = {}
        self.waited = {e: {} for e in ENGS}
        self.freed = {}
        self.epoch = 0
        self.nuniq = 0
        for e in ENGS:
            self._new_sem(e)

    def uniq(self, name):
        self.nuniq += 1
        return "%s_%d" % (name, self.nuniq)

    def _new_sem(self, e):
        key = "%s_e%d" % (e, self.epoch)
        sem = self.es.enter_context(self.nc.semaphore(key))
        self.sems[key] = (sem, 1)
        self.counts[key] = 0
        self.cur[e] = key

    def new_epoch(self):
        self.epoch += 1
        for e in ENGS:
            self._new_sem(e)

    def add_dma_sem(self, key):
        sem = self.es.enter_context(self.nc.semaphore(key))
        self.sems[key] = (sem, 16)
        self.counts[key] = 0

    def res(self):
        return Res(self.freed)

    def alias(self, rl):
        m = {}
        for r in rl:
            if r.w is not None and m.get(r.w[0], 0) < r.w[1]:
                m[r.w[0]] = r.w[1]
            for k, c in r.r.items():
                if m.get(k, 0) < c:
                    m[k] = c
        x = Res(self.freed)
        for k, c in m.items():
            if x.r.get(k, 0) < c:
                x.r[k] = c
        return x

    def free(self, r):
        f = self.freed
        if r.w is not None and f.get(r.w[0], 0) < r.w[1]:
            f[r.w[0]] = r.w[1]
        for k, c in r.r.items():
            if f.get(k, 0) < c:
                f[k] = c

    def emit(self, eng, fn, reads=(), writes=(), signal=True, dma=None):
        mykey = self.cur[eng]
        deps = {}
        for r in reads:
            t = r.w
            if t is not None:
                if t[0] == mykey and eng == "pe":
                    continue
                if deps.get(t[0], 0) < t[1]:
                    deps[t[0]] = t[1]
            if r.excl:
                for k, c in r.r.items():
                    if k != mykey and deps.get(k, 0) < c:
                        deps[k] = c
        same_ok = (eng != "pe")
        for w in writes:
            t = w.w
            if t is not None and (t[0] != mykey or same_ok):
                if deps.get(t[0], 0) < t[1]:
                    deps[t[0]] = t[1]
            for k, c in w.r.items():
                if (k != mykey or same_ok) and deps.get(k, 0) < c:
                    deps[k] = c
        e = self.engs[eng]
        wd = self.waited[eng]
        for k, c in deps.items():
            if wd.get(k, 0) < c:
                wd[k] = c
                sem, step = self.sems[k]
                e.wait_ge(sem, c * step)
        sigkey = dma if dma is not None else mykey
        if signal:
            self.counts[sigkey] += 1
            tok = (sigkey, self.counts[sigkey])
        else:
            tok = (sigkey, self.counts[sigkey] + 1)
        for r in reads:
            if r.r.get(tok[0], 0) < tok[1]:
                r.r[tok[0]] = tok[1]
        for w in writes:
            w.w = tok
            w.r = {}
        ins = fn(e)
        if signal:
            sem, step = self.sems[sigkey]
            ins.then_inc(sem, step)
        return tok

    def emit_dma(self, eng, fn, reads=(), writes=()):
        key = self.rr[self.rri % len(self.rr)]
        self.rri += 1
        if key in self.rrlast:
            self.wait_tokens(eng, [self.rrlast[key]])
        tok = self.emit(eng, fn, reads=reads, writes=writes, dma=key)
        self.rrlast[key] = tok
        return tok

    def wait_tokens(self, eng, toks):
        e = self.engs[eng]
        wd = self.waited[eng]
        for k, c in toks:
            if wd.get(k, 0) < c:
                wd[k] = c
                sem, step = self.sems[k]
                e.wait_ge(sem, c * step)


class Scope:
    def __init__(self, P):
        self.P = P
        self.es = ExitStack()
        self.rl = []

    def __enter__(self):
        self.es.__enter__()
        return self

    def __exit__(self, *a):
        for r in self.rl:
            self.P.free(r)
        return self.es.__exit__(*a)

    def sb(self, name, shape, dt):
        return self.es.enter_context(self.P.nc.sbuf_tensor(self.P.uniq(name), shape, dt))

    def R(self):
        r = self.P.res()
        self.rl.append(r)
        return r


def i_act(out, in_, func, bias=None, scale=1.0):
    if bias is None:
        return lambda e: e.activation(out=out, in_=in_, func=func, scale=scale)
    return lambda e: e.activation(out=out, in_=in_, func=func, bias=bias, scale=scale)


def i_tt(out, in0, in1, op):
    return lambda e: e.tensor_tensor(out=out, in0=in0, in1=in1, op=op)


def i_ts(out, in0, s1, s2, op0, op1=None):
    if op1 is None:
        return lambda e: e.tensor_scalar(out=out, in0=in0, scalar1=s1, scalar2=None, op0=op0)
    return lambda e: e.tensor_scalar(out=out, in0=in0, scalar1=s1, scalar2=s2, op0=op0, op1=op1)


def i_stt(out, in0, scalar, in1, op0, op1):
    return lambda e: e.scalar_tensor_tensor(out=out, in0=in0, scalar=scalar, in1=in1, op0=op0, op1=op1)


def i_copy(out, in_):
    return lambda e: e.tensor_copy(out=out, in_=in_)


def i_dma(out, in_, nc=None):
    return lambda e: e.dma_start(out=out, in_=in_)


def i_mm(out, lhsT, rhs, start, stop):
    return lambda e: e.matmul(out, lhsT=lhsT, rhs=rhs, start=start, stop=stop)


def i_tr(out, in_, ident):
    return lambda e: e.transpose(out=out, in_=in_, identity=ident)


def build_nc():
    nc = bass.Bass("TRN2", target_bir_lowering=False)

    def din(name, shape):
        return nc.dram_tensor(name, shape, F32, kind="ExternalInput").ap()

    def dout(name, shape):
        return nc.dram_tensor(name, shape, F32, kind="ExternalOutput").ap()

    xp = din("x_prompt", [SEQ, D])
    xs = din("x_sample", [SB, D])
    stc = din("state_conv", [NL, SB, 2, D])
    W = {name: din(name, shape) for name, shape in W_SHAPES}
    yp = dout("y_prompt", [SEQ, D])
    ys = dout("y_sample", [SB, D])
    ncp = dout("new_conv_prompt", [NL, 2, D])
    ncs = dout("new_conv_sample", [NL, SB, 2, D])
    ncv = dout("new_chunk_v", [NL, SB, D])

    with ExitStack() as es:
        P = Prog(nc, es)
        G = Scope(P)
        es.enter_context(G)
        es.enter_context(nc.allow_non_contiguous_dma(reason="small strided vectors"))

        h = G.sb("h", [128, 8, NTOK], F32)
        hR = [[G.R() for _ in TT] for _ in range(8)]
        n = G.sb("n", [128, 8, NTOK], BF16)
        nR = [[G.R() for _ in range(8)] for _ in TT]
        ring = G.sb("ring", [128, NSLOT, 1024], BF16)
        ident = G.sb("ident", [128, 128], F32)
        onesm = G.sb("onesm", [128, 128], F32)
        ones1 = G.sb("ones1", [128, 128], F32)
        epsb = G.sb("epsb", [128, 1], F32)
        mhalf = G.sb("mhalf", [128, 1], F32)
        cR = G.R()
        cv = G.sb("cv", [128, NL, 112], F32)
        hb = G.sb("hb", [128, NL, 16], F32)
        cvR = G.R()
        wTb = G.sb("wTb", [128, 4, 128], BF16)
        Rbc = G.sb("Rbc", [128, 4, 128], F32)
        bsbc = G.sb("bsbc", [128, 4, 128], F32)
        w00 = G.sb("w00", [SB, 4], F32)
        Dg = G.sb("Dg", [SB, 4, SB], BF16)
        S01 = G.sb("S01", [128, 2, 8, SB], F32)
        lsR = G.R()
        bsR = G.R()
        w0R = G.R()
        carry = G.sb("carry", [128, NL, 8, 2], F32)
        carR = [G.R() for _ in range(NL)]
        xgs = G.sb("xgs", [128, 8, SB], F32)
        xgsR = G.R()

        banks = [es.enter_context(nc.psum_tensor("pb%d" % i, [128, 512], F32)) for i in range(8)]
        bankR = [G.R() for _ in range(8)]
        for r_ in bankR:
            r_.excl = True
        bstate = [0]

        def next_bank():
            i = bstate[0]
            bstate[0] = (i + 1) % 8
            return banks[i], bankR[i]

        for key in ["xs0", "xs1", "xs2", "xs3", "xs4", "xs5", "os0", "os1", "os2"]:
            P.add_dma_sem(key)
        P.rr = ["rr%d" % i for i in range(20)]
        P.rri = 0
        P.rrlast = {}
        for key in P.rr:
            P.add_dma_sem(key)
        for s in range(NSLOT):
            P.add_dma_sem("w%d" % s)

        def mm_group(out_ap, bR, pairs, reads):
            nn = len(pairs)
            for i, (l, r) in enumerate(pairs):
                P.emit("pe", i_mm(out_ap, l, r, i == 0, i == nn - 1),
                       reads=reads if i == 0 else (), writes=[bR], signal=(i == nn - 1))

        specs = []

        def col(name, l, q):
            specs.append((W[name][l, :, q * 128:(q + 1) * 128].rearrange("(k p) c -> p k c", p=128), "col"))

        def row(name, l, r0, c0):
            specs.append((W[name][l, r0:r0 + 128, c0:c0 + 1024], "row"))

        def ffn_specs(l, pre):
            for (j0, j1) in GROUPS:
                for j in range(j0, j1):
                    col(pre + "_w_gate", l, j)
                    col(pre + "_w_up", l, j)
                for j in range(j0, j1):
                    row(pre + "_w_down", l, j * 128, 0)

        for l in range(NL):
            ffn_specs(l, "ffn1")
            for half in range(2):
                for k in range(8):
                    row("w_in", l, k * 128, 1024)
                for f in range(8):
                    for base in (0, 24, 32, 16, 40, 48):
                        col("w_in", l, base + f)
                for c in range(8):
                    col("w_out", l, c)
            ffn_specs(l, "ffn2")

        class WQ:
            def __init__(self):
                self.res = [G.R() for _ in range(NSLOT)]
                self.issued = 0
                self.got = 0
                self.released = [False] * len(specs)

            def pump(self):
                while self.issued < len(specs):
                    i = self.issued
                    if i >= NSLOT and not self.released[i - NSLOT]:
                        break
                    s = i % NSLOT
                    src, kind = specs[i]
                    if kind == "row":
                        dst = ring[:, s, :]
                    else:
                        dst = ring[:, s, :].rearrange("p (k c) -> p k c", k=8)
                    P.emit("pool", i_dma(dst, src), writes=[self.res[s]], dma="w%d" % s)
                    self.issued += 1

            def get(self):
                i = self.got
                self.got += 1
                self.pump()
                assert i < self.issued, "weight ring stalled"
                return i, i % NSLOT

            def release(self, i):
                self.released[i] = True
                self.pump()

        wq = WQ()

        def wcol(s):
            return ring[:, s, :].rearrange("p (k c) -> p k c", k=8)

        P.emit("pool", lambda e: e.memset(ones1[:], 1.0), writes=[cR])
        P.emit("pool", lambda e: e.memset(onesm[:], 1.0 / D), writes=[cR])
        P.emit("pool", lambda e: e.memset(epsb[:], EPS), writes=[cR])
        P.emit("pool", lambda e: e.memset(mhalf[:], -0.5), writes=[cR])
        P.emit("pool", lambda e: e.memset(carry[:], 0.0), writes=carR)
        P.emit("pool", lambda e: e.affine_select(out=ident[:], in_=ones1[:], pattern=[[-1, 128]],
                                                 compare_op=ALU.is_equal, fill=0.0, base=0,
                                                 channel_multiplier=1), reads=[cR], writes=[cR])

        class NormPipe:
            def __init__(self, sc, l, gcol, out_fn=None, final=None):
                self.final = final
                self.sq = sc.sb("sq", [128, 8, 512], F32)
                self.sqR = sc.R()
                self.ssq = [sc.sb("ssq", [128, 512], F32) for _ in range(3)]
                self.ssqR = [sc.R() for _ in range(3)]
                self.rs = [sc.sb("rs", [128, 512], F32) for _ in range(3)] + [sc.sb("rss", [128, SB], F32)]
                self.rsR = [sc.R() for _ in range(4)]
                self.l = l
                self.gcol = gcol
                self.k1 = 0
                self.k2 = 0
                self.slot = {}
                self.rslot = {}
                self.out_fn = out_fn

            def square(self, ti):
                t0, sz = TT[ti]
                P.emit("act", i_act(self.sq[:, :, 0:sz], h[:, :, t0:t0 + sz], AF.Square),
                       reads=[hR[c][ti] for c in range(8)], writes=[self.sqR])

            def square_h(self, ti, hf):
                t0, sz = TT[ti]
                P.emit("act", i_act(self.sq[:, hf * 4:hf * 4 + 4, 0:sz], h[:, hf * 4:hf * 4 + 4, t0:t0 + sz], AF.Square),
                       reads=[hR[c][ti] for c in range(hf * 4, hf * 4 + 4)], writes=[self.sqR])

            def adds_h(self, ti, hf):
                t0, sz = TT[ti]
                q = self.sq
                qR = self.sqR
                o = hf * 4
                P.emit("dve", i_tt(q[:, o:o + 2, 0:sz], q[:, o:o + 2, 0:sz], q[:, o + 2:o + 4, 0:sz], ALU.add), reads=[qR], writes=[qR])
                P.emit("dve", i_tt(q[:, o, 0:sz], q[:, o, 0:sz], q[:, o + 1, 0:sz], ALU.add), reads=[qR], writes=[qR])
                if hf == 1:
                    p = self.k1 % 3
                    self.k1 += 1
                    self.slot[ti] = p
                    P.emit("dve", i_tt(self.ssq[p][:, 0:sz], q[:, 0, 0:sz], q[:, 4, 0:sz], ALU.add), reads=[qR], writes=[self.ssqR[p]])

            def adds(self, ti):
                t0, sz = TT[ti]
                q = self.sq
                qR = self.sqR
                p = self.k1 % 3
                self.k1 += 1
                self.slot[ti] = p
                P.emit("dve", i_tt(q[:, 0:4, 0:sz], q[:, 0:4, 0:sz], q[:, 4:8, 0:sz], ALU.add), reads=[qR], writes=[qR])
                P.emit("dve", i_tt(q[:, 0:2, 0:sz], q[:, 0:2, 0:sz], q[:, 2:4, 0:sz], ALU.add), reads=[qR], writes=[qR])
                P.emit("dve", i_tt(self.ssq[p][:, 0:sz], q[:, 0, 0:sz], q[:, 1, 0:sz], ALU.add), reads=[qR], writes=[self.ssqR[p]])

            def mm(self, ti):
                t0, sz = TT[ti]
                p = self.slot[ti]
                if sz == 512:
                    r = self.k2 % 3
                    self.k2 += 1
                else:
                    r = 3
                self.rslot[ti] = r
                rs = self.rs[r]
                rR = self.rsR[r]
                if self.final is not None:
                    rs = self.final[0][:, t0:t0 + sz]
                    rR = self.final[1][ti]
                b, bR = next_bank()
                P.emit("pe", i_mm(b[:, 0:sz], onesm[:, :], self.ssq[p][:, 0:sz], True, True), reads=[self.ssqR[p], cR], writes=[bR])
                P.emit("act", i_act(rs[:, 0:sz], b[:, 0:sz], AF.Ln, bias=epsb[:, 0:1]), reads=[bR, cR], writes=[rR])
                P.emit("act", i_act(rs[:, 0:sz], rs[:, 0:sz], AF.Exp, scale=-0.5), reads=[rR], writes=[rR])

            def out(self, ti):
                if self.final is not None:
                    return
                t0, sz = TT[ti]
                r = self.rslot[ti]
                rs = self.rs[r]
                rR = self.rsR[r]
                if self.out_fn is not None:
                    self.out_fn(ti, rs, rR)
                    return
                for c in range(8):
                    self.scale_out(n[:, c, t0:t0 + sz], h[:, c, t0:t0 + sz], cv[:, self.l, self.gcol + c:self.gcol + c + 1],
                                   rs[:, 0:sz], sz, c, [hR[c][ti], rR, cvR], [nR[ti][c]])

            def scale_out(self, out, hin, g, rs_ap, sz, c, reads, writes):
                P.emit("dve", i_stt(out, hin, g, rs_ap, ALU.mult, ALU.mult), reads=reads, writes=writes)

        def run_tail(npipe, tiles, body):
            K = len(tiles)
            st = {"outed": 0}
            for idx, ti in enumerate(tiles):
                big = TT[ti][1] == 512

                def mid(ti=ti, idx=idx, big=big):
                    npipe.square_h(ti, 0)
                    if big and idx >= 2:
                        npipe.out(tiles[idx - 2])
                        st["outed"] = idx - 1
                    npipe.adds_h(ti, 0)

                if body(ti, mid):
                    npipe.square_h(ti, 1)
                    npipe.adds_h(ti, 1)
                else:
                    npipe.square(ti)
                    if big and idx >= 2:
                        npipe.out(tiles[idx - 2])
                        st["outed"] = idx - 1
                    npipe.adds(ti)
                if idx >= 1:
                    npipe.mm(tiles[idx - 1])
            npipe.mm(tiles[-1])
            for j in range(st["outed"], K):
                npipe.out(tiles[j])

        def load_cv(sc, defer):
            for l in range(NL):
                stg = sc.sb("cvstg", [112, 128], F32)
                sR = [sc.R() for _ in range(6)]
                srcs = [
                    (0, 56, W["b_in"][l].rearrange("(r c) -> r c", c=128)),
                    (56, 24, W["conv_w"][l].rearrange("k (r c) -> (k r) c", c=128)),
                    (80, 8, W["ffn1_norm"][l].rearrange("(r c) -> r c", c=128)),
                    (88, 8, W["mix_norm"][l].rearrange("(r c) -> r c", c=128)),
                    (96, 8, W["ffn2_norm"][l].rearrange("(r c) -> r c", c=128)),
                    (104, 8, W["final_norm"].rearrange("(r c) -> r c", c=128)),
                ]
                for di, (r0, nr, src) in enumerate(srcs):
                    P.emit_dma("act", i_dma(stg[r0:r0 + nr, :], src), writes=[sR[di]])

                def comp(l=l, stg=stg, sR=sR):
                    b, bR = next_bank()
                    P.emit("pe", i_tr(b[:, 0:112], stg[:, :], ident[0:112, 0:112]), reads=sR + [cR], writes=[bR])
                    P.emit("dve", i_copy(cv[:, l, :], b[:, 0:112]), reads=[bR], writes=[cvR])
                    P.emit("dve", i_ts(hb[:, l, :], cv[:, l, 40:56], 0.5, None, ALU.mult), reads=[cvR], writes=[cvR])
                defer.append(comp)

        def prologue_x(sc, hook):
            if True:
                NXS = 6
                xst = [sc.sb("xst", [128, 1024], F32) for _ in range(NXS)]
                xR = [sc.R() for _ in range(NXS)]
                npipe = NormPipe(sc, 0, 80)
                st_ = {"ev": 0}

                def xbody(tt, mid=None):
                    if tt < 4:
                        for i in range(tt * 4, tt * 4 + 4):
                            s = i % NXS
                            tok = P.emit("sp", i_dma(xst[s][:, :], xp[i * 128:(i + 1) * 128, :]), writes=[xR[s]], dma="xs%d" % s)
                            if i == 9:
                                hook(0)
                            if i == 11:
                                P.wait_tokens("pool", [tok])
                                wq.pump()
                            for hf in range(2):
                                b, bR = next_bank()
                                for q in range(4):
                                    c = hf * 4 + q
                                    P.emit("pe", i_tr(b[:, q * 128:(q + 1) * 128], xst[s][:, c * 128:(c + 1) * 128], ident[:, :]),
                                           reads=[xR[s], cR], writes=[bR], signal=(q == 3))
                                dst = h[:, hf * 4:(hf + 1) * 4, i * 128:(i + 1) * 128]
                                src = b[:, :].rearrange("p (q t) -> p q t", q=4)
                                wr = [hR[hf * 4 + q][i // 4] for q in range(4)]
                                if st_["ev"] % 2 == 0 or i < 6:
                                    P.emit("dve", i_copy(dst, src), reads=[bR], writes=wr)
                                else:
                                    P.emit("act", i_act(dst, src, AF.Copy), reads=[bR], writes=wr)
                                st_["ev"] += 1
                    else:
                        hook(1)
                        s = 16 % NXS
                        P.emit("sp", i_dma(xst[s][0:SB, :], xs[:, :]), writes=[xR[s]], dma="xs%d" % s)
                        b, bR = next_bank()
                        for c in range(8):
                            P.emit("pe", i_tr(b[:, c * SB:(c + 1) * SB], xst[s][0:SB, c * 128:(c + 1) * 128], ident[0:SB, 0:SB]),
                                   reads=[xR[s], cR], writes=[bR], signal=(c == 7))
                        P.emit("dve", i_copy(h[:, :, SEQ:NTOK], b[:, 0:8 * SB].rearrange("p (c t) -> p c t", c=8)),
                               reads=[bR], writes=[hR[c][4] for c in range(8)])

                run_tail(npipe, [0, 1, 2, 3, 4], xbody)

        def ffn(l, pre, tail):
            with Scope(P) as sc:
                act = sc.sb("act", [128, 8, NTOK], BF16)
                actR = [[sc.R() for _ in TT] for _ in range(8)]
                stmp = [sc.sb("stmp", [128, 512], F32) for _ in range(2)]
                stR = [sc.R() for _ in range(2)]
                if tail == "final":
                    npipe = NormPipe(sc, 0, 104, final=(rs_all, rsF))
                else:
                    npipe = NormPipe(sc, tail[0], tail[1]) if tail is not None else None
                ui = 0
                for gi, (j0, j1) in enumerate(GROUPS):
                    for j in range(j0, j1):
                        ig, sg = wq.get()
                        iu, su = wq.get()
                        Wg = wcol(sg)
                        Wu = wcol(su)
                        for ti, (t0, sz) in enumerate(TT):
                            ba, bRa = next_bank()
                            mm_group(ba[:, 0:sz], bRa, [(Wg[:, k, :], n[:, k, t0:t0 + sz]) for k in range(8)],
                                     [wq.res[sg]] + nR[ti])
                            bb, bRb = next_bank()
                            mm_group(bb[:, 0:sz], bRb, [(Wu[:, k, :], n[:, k, t0:t0 + sz]) for k in range(8)],
                                     [wq.res[su]] + nR[ti])
                            st = stmp[ui % 2]
                            sR = stR[ui % 2]
                            P.emit("act", i_act(st[:, 0:sz], ba[:, 0:sz], AF.Silu), reads=[bRa], writes=[sR])
                            P.emit("dve", i_tt(act[:, j - j0, t0:t0 + sz], bb[:, 0:sz], st[:, 0:sz], ALU.mult),
                                   reads=[bRb, sR], writes=[actR[j - j0][ti]])
                            ui += 1
                        wq.release(ig)
                        wq.release(iu)
                    ds = [wq.get() for _ in range(j0, j1)]
                    last = (gi == len(GROUPS) - 1) and npipe is not None

                    def dbody(ti, mid=None, ds=ds):
                        t0, sz = TT[ti]
                        for c in range(8):
                            if c == 4 and mid is not None and sz == 512:
                                mid()
                            b, bR = next_bank()
                            mm_group(b[:, 0:sz], bR,
                                     [(ring[:, s, c * 128:(c + 1) * 128], act[:, jl, t0:t0 + sz]) for jl, (_, s) in enumerate(ds)],
                                     [wq.res[s] for (_, s) in ds] + [actR[jl][ti] for jl in range(len(ds))])
                            P.emit("dve", i_stt(h[:, c, t0:t0 + sz], b[:, 0:sz], 0.5, h[:, c, t0:t0 + sz], ALU.mult, ALU.add),
                                   reads=[bR, hR[c][ti]], writes=[hR[c][ti]])
                        return mid is not None and sz == 512

                    if last:
                        run_tail(npipe, list(range(len(TT))), dbody)
                    else:
                        for ti in range(len(TT)):
                            dbody(ti)
                    for (i, _) in ds:
                        wq.release(i)

        def layer_setup(l, sc, q, defer):
            if True:
                wst = sc.sb("wst", [128, 4, 128], F32)
                wT32 = sc.sb("wT32", [128, 4, 128], F32)
                sts = sc.sb("sts", [SB, 2, D], F32)
                tR = sc.R()
                t2R = sc.R()
                sR = sc.R()
                P.emit_dma(q, i_dma(wst[:, :, :], W["w_spatial"][l].rearrange("g t s -> t g s")), writes=[tR])
                P.emit_dma(q, i_dma(bsbc[:, :, :].rearrange("p g t -> p (g t)"),
                                   W["b_spatial"][l].rearrange("g t -> (g t)").partition_broadcast(128)),
                       writes=[bsR])
                P.emit_dma(q, i_dma(w00[:, :], W["w_spatial"][l, :, 0, 0].partition_broadcast(SB)), writes=[w0R])
                P.emit_dma(q, i_dma(sts[:, :, :], stc[l]), writes=[sR])
                P.emit_dma(q, i_dma(ncs[l, :, 0, :], stc[l, :, 1, :]))
                defer.append(lambda: ls_compute(l, wst, wT32, sts, tR, t2R, sR))

        def ls_compute(l, wst, wT32, sts, tR, t2R, sR):
            if True:
                P.emit("pool", lambda e: e.affine_select(out=wst[:, :, :], in_=wst[:, :, :], pattern=[[0, 4], [-1, 128]],
                                                         compare_op=ALU.is_ge, fill=0.0, base=0, channel_multiplier=1),
                       reads=[tR], writes=[tR])
                b, bR = next_bank()
                for g in range(4):
                    P.emit("pe", i_tr(b[:, g * 128:(g + 1) * 128], wst[:, g, :], ident[:, :]), reads=[tR, cR], writes=[bR],
                           signal=(g == 3))
                P.emit("dve", i_copy(wT32[:, :, :], b[:, :].rearrange("p (g t) -> p g t", g=4)), reads=[bR], writes=[t2R])
                P.emit("act", i_act(wTb[:, :, :], wT32[:, :, :], AF.Copy), reads=[t2R], writes=[lsR])
                b2, b2R = next_bank()
                P.emit("pe", i_mm(b2[:, :], ones1[:, :], wT32[:, :, :].rearrange("p g t -> p (g t)"), True, True),
                       reads=[t2R, cR], writes=[b2R])
                P.emit("dve", i_copy(Rbc[:, :, :].rearrange("p g t -> p (g t)"), b2[:, :]), reads=[b2R], writes=[lsR])
                for g in range(4):
                    P.emit("dve", i_ts(Dg[:, g, :], ident[0:SB, 0:SB], w00[:, g:g + 1], None, ALU.mult),
                           reads=[lsR, cR, w0R], writes=[lsR])
                b3, b3R = next_bank()
                for k in range(2):
                    for c in range(8):
                        P.emit("pe", i_tr(b3[:, (k * 8 + c) * SB:(k * 8 + c + 1) * SB], sts[:, k, c * 128:(c + 1) * 128],
                                          ident[0:SB, 0:SB]), reads=[sR, cR], writes=[b3R], signal=(k == 1 and c == 7))
                P.emit("dve", i_copy(S01[:, :, :, :].rearrange("p k c t -> p (k c t)"), b3[:, 0:256]), reads=[b3R], writes=[lsR])

        def mixer(l, tail):
            for half in range(2):
                tis = [0, 1] if half == 0 else [2, 3, 4]
                mcol = {0: 0, 1: 512, 2: 0, 3: 512, 4: 1024}
                with Scope(P) as sc:
                    vln = sc.sb("vln", [128, 9, 1024], BF16)
                    vlnR = [sc.R() for _ in range(9)]
                    m = sc.sb("m", [128, 8, 1040], BF16)
                    mR = {ti: [sc.R() for _ in range(8)] for ti in tis}
                    with Scope(P) as sa:
                        NVB = 3
                        vb = [sa.sb("vb", [128, 1024], F32) for _ in range(NVB)]
                        vbR = [sa.R() for _ in range(NVB)]
                        bvbc = sa.sb("bvbc", [128, 1024], F32)
                        bvR = sa.R()
                        st12 = sa.sb("st12", [128, NVB, 12], F32)
                        junk = sa.sb("junk", [128, 1024], BF16)
                        jR = sa.R()
                        mv = sa.sb("mv", [128, NVB, 4], F32)
                        smR = [sa.R() for _ in range(NVB)]
                        P.emit_dma("sp", i_dma(bvbc[:, :], W["b_in"][l, 1024:2048].partition_broadcast(128)), writes=[bvR])
                        if half == 1:
                            gb16 = sa.sb("gb16", [SB, 2, 1024], F32)
                            gbR = sa.R()
                            P.emit_dma("sp", i_dma(gb16[:, 0, :], W["v_ln_gain"][l].partition_broadcast(SB)), writes=[gbR])
                            P.emit_dma("sp", i_dma(gb16[:, 1, :], W["v_ln_bias"][l].partition_broadcast(SB)), writes=[gbR])
                        wv = [wq.get() for _ in range(8)]
                        tiles = [(half * 1024 + i * 128, 128, i) for i in range(8)]
                        if half == 1:
                            tiles.insert(0, (SEQ, SB, 8))
                        def stage_a(ix):
                            t0, sz, vi = tiles[ix]
                            ti = 4 if sz == SB else t0 // 512
                            p = ix % NVB
                            v_ = vb[p]
                            vR = vbR[p]
                            for hh in range(2):
                                b, bR = next_bank()
                                mm_group(b[0:sz, :], bR,
                                         [(n[:, k, t0:t0 + sz], ring[:, wv[k][1], hh * 512:(hh + 1) * 512]) for k in range(8)],
                                         [wq.res[s] for (_, s) in wv] + nR[ti])
                                P.emit("dve", i_tt(v_[0:sz, hh * 512:(hh + 1) * 512], b[0:sz, :], bvbc[0:sz, hh * 512:(hh + 1) * 512], ALU.add),
                                       reads=[bR, bvR], writes=[vR])
                            P.emit("act", lambda e, o=v_[0:sz, :], a=st12[0:sz, p, 0:1]: e.activation(out=o, in_=o, func=AF.Gelu, accum_out=a),
                                   reads=[vR], writes=[vR, smR[p]])
                            P.emit("act", lambda e, o=junk[0:sz, :], i=v_[0:sz, :], a=st12[0:sz, p, 1:2]:
                                   e.activation(out=o, in_=i, func=AF.Square, accum_out=a),
                                   reads=[vR], writes=[jR, smR[p]])

                        def stage_b(ix):
                            t0, sz, vi = tiles[ix]
                            p = ix % NVB
                            v_ = vb[p]
                            vR = vbR[p]
                            P.emit("dve", i_ts(mv[0:sz, p, 0:1], st12[0:sz, p, 0:1], 1.0 / D, None, ALU.mult), reads=[smR[p]], writes=[smR[p]])
                            P.emit("dve", i_tt(mv[0:sz, p, 1:2], mv[0:sz, p, 0:1], mv[0:sz, p, 0:1], ALU.mult), reads=[smR[p]], writes=[smR[p]])
                            P.emit("dve", i_stt(mv[0:sz, p, 3:4], st12[0:sz, p, 1:2], 1.0 / D, mv[0:sz, p, 1:2], ALU.mult, ALU.subtract),
                                   reads=[smR[p]], writes=[smR[p]])
                            P.emit("dve", i_ts(mv[0:sz, p, 3:4], mv[0:sz, p, 3:4], EPS, None, ALU.add), reads=[smR[p]], writes=[smR[p]])
                            P.emit("pool", i_tt(mv[0:sz, p, 2:3], mv[0:sz, p, 3:4], mhalf[0:sz, 0:1], ALU.pow),
                                   reads=[smR[p], cR], writes=[smR[p]])

                        def stage_c(ix):
                            t0, sz, vi = tiles[ix]
                            p = ix % NVB
                            v_ = vb[p]
                            vR = vbR[p]
                            if sz == 128:
                                P.emit("dve", i_ts(vln[:, vi, :], v_[:, :], mv[:, p, 0:1], mv[:, p, 2:3], ALU.subtract, ALU.mult),
                                       reads=[vR, smR[p]], writes=[vlnR[vi]])
                            else:
                                P.emit("dve", i_ts(v_[0:sz, :], v_[0:sz, :], mv[0:sz, p, 0:1], mv[0:sz, p, 2:3], ALU.subtract, ALU.mult),
                                       reads=[vR, smR[p]], writes=[vR])
                                P.emit("act", i_act(vln[0:sz, vi, :], v_[0:sz, :], AF.Copy), reads=[vR], writes=[vlnR[vi]])
                                P.emit("dve", i_tt(v_[0:sz, :], v_[0:sz, :], gb16[:, 0, :], ALU.mult), reads=[vR, gbR], writes=[vR])
                                P.emit("dve", i_tt(v_[0:sz, :], v_[0:sz, :], gb16[:, 1, :], ALU.add), reads=[vR, gbR], writes=[vR])
                                P.emit_dma("sp", i_dma(ncv[l], v_[0:sz, :]), reads=[vR])

                        NT_ = len(tiles)
                        for ix in range(NT_):
                            stage_a(ix)
                            if ix >= 1:
                                stage_b(ix - 1)
                            if ix >= 2:
                                stage_c(ix - 2)
                        stage_b(NT_ - 1)
                        stage_c(NT_ - 2)
                        stage_c(NT_ - 1)
                        for (i, _) in wv:
                            wq.release(i)
                    with Scope(P) as sbx:
                        NTMP = 2
                        tmp = {nm: [sbx.sb(nm, [128, 512], F32) for _ in range(NTMP if nm in ("u", "c", "ta", "tb") else 1)]
                               for nm in ("u", "c", "ta", "tb", "ya", "yb", "cv1", "zz")}
                        tmpR = {nm: [sbx.R() for _ in tmp[nm]] for nm in tmp}
                        xg = sbx.sb("xg", [128, 1026], F32)
                        xgR = sbx.R()
                        Ef = sbx.sb("Ef", [128, 128], F32)
                        EfR = sbx.R()
                        ui = 0
                        for f in range(8):
                            g = f // 2
                            ws = [wq.get() for _ in range(6)]
                            (iu_, su_), (ic_, sc_), (ix_, sx_), (ib_, sb__), (iga, sga), (igb, sgb) = ws
                            gain_f = cv
                            P.emit("dve", i_stt(Ef[:, :], Rbc[:, g, :], lnb[:, l, f:f + 1], bsbc[:, g, :], ALU.mult, ALU.add),
                                   reads=[lsR, lnR, bsR], writes=[EfR])
                            P.emit("dve", i_copy(xg[:, 0:2], carry[:, l, f, :]), reads=[carR[l]], writes=[xgR])
                            for ti in tis:
                                t0, sz = TT[ti]
                                mc = mcol[ti]
                                p = ui % NTMP
                                ui += 1
                                T = {nm: tmp[nm][p % len(tmp[nm])] for nm in tmp}
                                TR = {nm: tmpR[nm][p % len(tmp[nm])] for nm in tmp}
                                nrd = nR[ti]

                                def lin(slot):
                                    b, bR = next_bank()
                                    Wc = wcol(slot)
                                    mm_group(b[:, 0:sz], bR, [(Wc[:, k, :], n[:, k, t0:t0 + sz]) for k in range(8)],
                                             [wq.res[slot]] + nrd)
                                    return b, bR

                                b, bR = lin(su_)
                                P.emit("act", i_act(T["u"][:, 0:sz], b[:, 0:sz], AF.Gelu, bias=cv[:, l, f:f + 1]),
                                       reads=[bR, cvR], writes=[TR["u"]])
                                b, bR = next_bank()
                                if sz == 512:
                                    for c4 in range(4):
                                        vi = (t0 - half * 1024) // 128 + c4
                                        P.emit("pe", i_mm(b[:, c4 * 128:(c4 + 1) * 128], vln[:, vi, f * 128:(f + 1) * 128], wTb[:, g, :], True, True),
                                               reads=[vlnR[vi], lsR], writes=[bR], signal=(c4 == 3))
                                    P.emit("dve", i_stt(T["zz"][:, :].rearrange("p (a t) -> p a t", a=4),
                                                        b[:, :].rearrange("p (a t) -> p a t", a=4), lng[:, l, f:f + 1],
                                                        Ef[:, :].unsqueeze(1).to_broadcast([128, 4, 128]), ALU.mult, ALU.add),
                                           reads=[bR, EfR, lnR], writes=[TR["zz"]])
                                else:
                                    P.emit("pe", i_mm(b[:, 0:sz], vln[0:SB, 8, f * 128:(f + 1) * 128], Dg[:, g, :], True, True),
                                           reads=[vlnR[8], lsR], writes=[bR])
                                    P.emit("dve", i_stt(T["zz"][:, 0:sz], b[:, 0:sz], lng[:, l, f:f + 1],
                                                        Ef[:, 0:1].to_broadcast([128, sz]), ALU.mult, ALU.add),
                                           reads=[bR, EfR, lnR], writes=[TR["zz"]])
                                P.emit("dve", i_tt(T["ya"][:, 0:sz], T["zz"][:, 0:sz], T["u"][:, 0:sz], ALU.mult),
                                       reads=[TR["zz"], TR["u"]], writes=[TR["ya"]])
                                b, bR = lin(sc_)
                                P.emit("act", i_act(T["c"][:, 0:sz], b[:, 0:sz], AF.Identity, bias=cv[:, l, 24 + f:25 + f]),
                                       reads=[bR, cvR], writes=[TR["c"]])
                                b, bR = lin(sx_)
                                if sz == 512:
                                    xo = 2 + mc
                                    P.emit("dve", i_stt(xg[:, xo:xo + sz], b[:, 0:sz], cv[:, l, 32 + f:33 + f], T["c"][:, 0:sz], ALU.add, ALU.mult),
                                           reads=[bR, TR["c"], cvR], writes=[xgR])
                                    x0 = xg[:, mc:mc + sz]
                                    x1 = xg[:, mc + 1:mc + 1 + sz]
                                    x2 = xg[:, mc + 2:mc + 2 + sz]
                                    xrd = [xgR]
                                else:
                                    P.emit("dve", i_stt(xgs[:, f, :], b[:, 0:sz], cv[:, l, 32 + f:33 + f], T["c"][:, 0:sz], ALU.add, ALU.mult),
                                           reads=[bR, TR["c"], cvR], writes=[xgsR])
                                    x0 = S01[:, 0, f, :]
                                    x1 = S01[:, 1, f, :]
                                    x2 = xgs[:, f, :]
                                    xrd = [xgsR, lsR]
                                cvt = T["cv1"]
                                P.emit("dve", i_ts(cvt[:, 0:sz], x0, cv[:, l, 56 + f:57 + f], None, ALU.mult),
                                       reads=xrd + [cvR], writes=[TR["cv1"]])
                                P.emit("dve", i_stt(cvt[:, 0:sz], x1, cv[:, l, 64 + f:65 + f], cvt[:, 0:sz], ALU.mult, ALU.add),
                                       reads=xrd + [cvR, TR["cv1"]], writes=[TR["cv1"]])
                                P.emit("dve", i_stt(cvt[:, 0:sz], x2, cv[:, l, 72 + f:73 + f], cvt[:, 0:sz], ALU.mult, ALU.add),
                                       reads=xrd + [cvR, TR["cv1"]], writes=[TR["cv1"]])
                                b, bR = lin(sb__)
                                P.emit("dve", i_stt(T["yb"][:, 0:sz], b[:, 0:sz], cv[:, l, 16 + f:17 + f], cvt[:, 0:sz], ALU.add, ALU.mult),
                                       reads=[bR, TR["cv1"], cvR], writes=[TR["yb"]])
                                b, bR = lin(sga)
                                P.emit("act", i_act(T["ta"][:, 0:sz], b[:, 0:sz], AF.Tanh, bias=hb[:, l, f:f + 1], scale=0.5),
                                       reads=[bR, cvR], writes=[TR["ta"]])
                                b, bR = lin(sgb)
                                P.emit("act", i_act(T["tb"][:, 0:sz], b[:, 0:sz], AF.Tanh, bias=hb[:, l, 8 + f:9 + f], scale=0.5),
                                       reads=[bR, cvR], writes=[TR["tb"]])
                                P.emit("dve", i_stt(T["ya"][:, 0:sz], T["ta"][:, 0:sz], 1.0, T["ya"][:, 0:sz], ALU.add, ALU.mult),
                                       reads=[TR["ta"], TR["ya"]], writes=[TR["ya"]])
                                P.emit("dve", i_stt(T["yb"][:, 0:sz], T["tb"][:, 0:sz], 1.0, T["yb"][:, 0:sz], ALU.add, ALU.mult),
                                       reads=[TR["tb"], TR["yb"]], writes=[TR["yb"]])
                                P.emit("dve", i_tt(m[:, f, mc:mc + sz], T["ya"][:, 0:sz], T["yb"][:, 0:sz], ALU.add),
                                       reads=[TR["ya"], TR["yb"]], writes=[mR[ti][f]])
                            P.emit("dve", i_copy(carry[:, l, f, :], xg[:, 1024:1026]), reads=[xgR], writes=[carR[l]])
                            for (i, _) in ws:
                                wq.release(i)
                    if half == 1:
                        with Scope(P) as so_:
                            o16 = so_.sb("o16", [SB, D], F32)
                            oR = so_.R()
                            b0, b0R = next_bank()
                            b1, b1R = next_bank()
                            for c in range(8):
                                bb_, bbR = (b0, b0R) if c < 4 else (b1, b1R)
                                P.emit("pe", i_tr(bb_[0:SB, (c % 4) * 128:(c % 4 + 1) * 128], xgs[:, c, :], ident[:, :]),
                                       reads=[xgsR, cR], writes=[bbR], signal=(c % 4 == 3))
                            P.emit("dve", i_copy(o16[:, 0:512], b0[0:SB, :]), reads=[b0R], writes=[oR])
                            P.emit("dve", i_copy(o16[:, 512:1024], b1[0:SB, :]), reads=[b1R], writes=[oR])
                            tok = P.emit_dma("sp", i_dma(ncs[l, :, 1, :], o16[:, :]), reads=[oR])
                            P.wait_tokens("sp", [tok])
                    with Scope(P) as scC:
                        npipe = NormPipe(scC, tail[0], tail[1])
                        wo = [wq.get() for _ in range(8)]
                        def cbody(ti, mid=None):
                            t0, sz = TT[ti]
                            mc = mcol[ti]
                            for c in range(8):
                                if c == 4 and mid is not None and sz == 512:
                                    mid()
                                io, so = wo[c]
                                Wc = wcol(so)
                                b, bR = next_bank()
                                mm_group(b[:, 0:sz], bR, [(Wc[:, k, :], m[:, k, mc:mc + sz]) for k in range(8)],
                                         [wq.res[so]] + mR[ti])
                                P.emit("dve", i_stt(h[:, c, t0:t0 + sz], b[:, 0:sz], 0.5, h[:, c, t0:t0 + sz], ALU.mult, ALU.add),
                                       reads=[bR, hR[c][ti]], writes=[hR[c][ti]])
                            return mid is not None and sz == 512

                        run_tail(npipe, tis, cbody)
                        for (io, _) in wo:
                            wq.release(io)
            for k in range(2):
                P.emit_dma("sp", i_dma(ncp[l, k, :].rearrange("(f p) -> p f", p=128), carry[:, l, :, k]), reads=[carR[l]])

        lng = G.sb("lng", [128, NL, 8], F32)
        lnb = G.sb("lnb", [128, NL, 8], F32)
        lnR = G.R()
        def load_ln(sc, defer):
            stg = sc.sb("lnstg", [32, 128], F32)
            sR = sc.R()
            sR2 = sc.R()
            P.emit_dma("act", i_dma(stg[0:16, :], W["v_ln_gain"].rearrange("l (r c) -> (l r) c", c=128)), writes=[sR])
            P.emit_dma("act", i_dma(stg[16:32, :], W["v_ln_bias"].rearrange("l (r c) -> (l r) c", c=128)), writes=[sR2])

            def comp():
                b, bR = next_bank()
                P.emit("pe", i_tr(b[:, 0:32], stg[:, :], ident[0:32, 0:32]), reads=[sR, sR2, cR], writes=[bR])
                P.emit("dve", i_copy(lng[:, :, :].rearrange("p l c -> p (l c)"), b[:, 0:16]), reads=[bR], writes=[lnR])
                P.emit("dve", i_copy(lnb[:, :, :].rearrange("p l c -> p (l c)"), b[:, 16:32]), reads=[bR], writes=[lnR])
            defer.append(comp)

        for l in range(NL):
            P.new_epoch()
            if l == 0:
                with Scope(P) as s0:
                    dfr = []
                    dfr2 = []
                    load_cv(s0, dfr)
                    load_ln(s0, dfr2)
                    layer_setup(0, s0, "act", dfr2)
                    prologue_x(s0, lambda k: [f() for f in (dfr if k == 0 else dfr2)])
            ffn(l, "ffn1", (l, 88))
            P.new_epoch()
            mixer(l, (l, 96))
            P.new_epoch()
            if l + 1 < NL:
                with Scope(P) as s1:
                    dfr = []
                    layer_setup(l + 1, s1, "sp", dfr)
                    for f in dfr:
                        f()
            if l + 1 < NL:
                ffn(l, "ffn2", (l + 1, 80))
            else:
                rs_all = n[:, 0:2, :].rearrange("p c t -> p (c t)").bitcast(F32)
                rsF = [P.alias([nR[ti][k] for ti in range(len(TT)) for k in range(8)]) for _ in TT]
                ffn(l, "ffn2", "final")

        P.new_epoch()
        with Scope(P) as sc:
            yT = [sc.sb("yT", [128, 8, 512], F32) for _ in range(2)]
            yR = [sc.R() for _ in range(2)]
            NOS = 3
            ost = [sc.sb("ost", [128, 1024], F32) for _ in range(NOS)]
            oR = [sc.R() for _ in range(NOS)]
            oc = 0
            for ti, (t0, sz) in enumerate(TT):
                y = yT[ti % 2]
                yr = yR[ti % 2]
                for c in range(8):
                    P.emit("dve", i_stt(y[:, c, 0:sz], h[:, c, t0:t0 + sz], cv[:, 0, 104 + c:105 + c], rs_all[:, t0:t0 + sz], ALU.mult, ALU.mult),
                           reads=[hR[c][ti], rsF[ti], cvR], writes=[yr])
                nb = 4 if sz == 512 else 1
                bw = 128 if sz == 512 else SB
                for bk in range(nb):
                    s_ = oc % NOS
                    oc += 1
                    for hf in range(2):
                        b, bR = next_bank()
                        for q in range(4):
                            c = hf * 4 + q
                            P.emit("pe", i_tr(b[0:bw, q * 128:(q + 1) * 128], y[:, c, bk * 128:bk * 128 + bw], ident[:, :]),
                                   reads=[yr, cR], writes=[bR], signal=(q == 3))
                        P.emit("act", i_act(ost[s_][0:bw, hf * 512:(hf + 1) * 512], b[0:bw, :], AF.Copy), reads=[bR], writes=[oR[s_]])
                    if sz == 512:
                        dst = yp[t0 + bk * 128:t0 + (bk + 1) * 128, :]
                    else:
                        dst = ys[:, :]
                    P.emit("sp", i_dma(dst, ost[s_][0:bw, :]), reads=[oR[s_]], dma="os%d" % s_)
            P.wait_tokens("sp", [(k, P.counts[k]) for k in P.rr + ["os0", "os1", "os2"]])
    return nc


_NC = None


def kernel(**inputs):
    global _NC
    if _NC is None:
        _NC = build_nc()
    nc = _NC
    f = lambda a: np.ascontiguousarray(np.asarray(a, dtype=np.float32))
    xp = f(inputs["x_prompt"])
    xs = f(inputs["x_sample"])
    st = f(inputs["state_conv"])
    shared = {name: f(inputs[name]) for name, _ in W_SHAPES}
    in_maps = []
    for c in range(NCORE):
        d = dict(shared)
        d["x_prompt"] = np.ascontiguousarray(xp[c])
        d["x_sample"] = np.ascontiguousarray(xs[c * SB:(c + 1) * SB, 0, :])
        d["state_conv"] = np.ascontiguousarray(st[:, c * SB:(c + 1) * SB])
        in_maps.append(d)
    res = run_bass_kernel_spmd(nc, in_maps, core_ids=list(range(NCORE)))
    R = res.results
    y_prompt = np.stack([R[c]["y_prompt"] for c in range(NCORE)], axis=0)
    y_sample = np.concatenate([R[c]["y_sample"] for c in range(NCORE)], axis=0)[:, None, :]
    ncp = np.stack([R[c]["new_conv_prompt"] for c in range(NCORE)], axis=1)
    ncs = np.concatenate([R[c]["new_conv_sample"] for c in range(NCORE)], axis=1)
    ncv = np.concatenate([R[c]["new_chunk_v"] for c in range(NCORE)], axis=1)[:, :, None, :]
    return (y_prompt.astype(np.float32), y_sample.astype(np.float32), ncp.astype(np.float32),
            ncs.astype(np.float32), ncv.astype(np.float32))
```

```python
import numpy as np
from contextlib import ExitStack
import concourse.bass as bass
import concourse.mybir as mybir
from concourse.bass_utils import run_bass_kernel_spmd

F32 = mybir.dt.float32
BF16 = mybir.dt.bfloat16
AF = mybir.ActivationFunctionType
ALU = mybir.AluOpType

D = 1024
DFF = 2816
NL = 2
NCORE = 8
SEQ = 2048
SB = 16
NTOK = SEQ + SB
INC = 7168
TT = [(0, 512), (512, 512), (1024, 512), (1536, 512), (2048, 16)]
EPS = 1e-6
NSLOT = 16
GROUPS = [(0, 6), (6, 14), (14, 22)]
ENGS = ("pe", "act", "dve", "pool", "sp")

W_SHAPES = [
    ("ffn1_norm", [NL, D]), ("ffn1_w_gate", [NL, D, DFF]), ("ffn1_w_up", [NL, D, DFF]),
    ("ffn1_w_down", [NL, DFF, D]), ("mix_norm", [NL, D]), ("w_in", [NL, D, INC]),
    ("b_in", [NL, INC]), ("v_ln_gain", [NL, D]), ("v_ln_bias", [NL, D]),
    ("w_spatial", [NL, 4, 128, 128]), ("b_spatial", [NL, 4, 128]), ("conv_w", [NL, 3, D]),
    ("w_out", [NL, D, D]), ("ffn2_norm", [NL, D]), ("ffn2_w_gate", [NL, D, DFF]),
    ("ffn2_w_up", [NL, D, DFF]), ("ffn2_w_down", [NL, DFF, D]), ("final_norm", [D]),
]


class Res:
    __slots__ = ("w", "r", "excl")

    def __init__(self, seed=None):
        self.w = None
        self.r = dict(seed) if seed else {}
        self.excl = False


class Prog:
    def __init__(self, nc, es):
        self.nc = nc
        self.es = es
        self.engs = {"pe": nc.tensor, "act": nc.scalar, "dve": nc.vector,
                     "pool": nc.gpsimd, "sp": nc.sync}
        self.sems = {}
        self.counts = {}
        self.cur = {}
        self.waited = {e: {} for e in ENGS}
        self.freed = {}
        self.epoch = 0
        self.nuniq = 0
        for e in ENGS:
            self._new_sem(e)

    def uniq(self, name):
        self.nuniq += 1
        return "%s_%d" % (name, self.nuniq)

    def _new_sem(self, e):
        key = "%s_e%d" % (e, self.epoch)
        sem = self.es.enter_context(self.nc.semaphore(key))
        self.sems[key] = (sem, 1)
        self.counts[key] = 0
        self.cur[e] = key

    def new_epoch(self):
        self.epoch += 1
        for e in ENGS:
            self._new_sem(e)

    def add_dma_sem(self, key):
        sem = self.es.enter_context(self.nc.semaphore(key))
        self.sems[key] = (sem, 16)
        self.counts[key] = 0

    def res(self):
        return Res(self.freed)

    def alias(self, rl):
        m = {}
        for r in rl:
            if r.w is not None and m.get(r.w[0], 0) < r.w[1]:
                m[r.w[0]] = r.w[1]
            for k, c in r.r.items():
                if m.get(k, 0) < c:
                    m[k] = c
        x = Res(self.freed)
        for k, c in m.items():
            if x.r.get(k, 0) < c:
                x.r[k] = c
        return x

    def free(self, r):
        f = self.freed
        if r.w is not None and f.get(r.w[0], 0) < r.w[1]:
            f[r.w[0]] = r.w[1]
        for k, c in r.r.items():
            if f.get(k, 0) < c:
                f[k] = c

    def emit(self, eng, fn, reads=(), writes=(), signal=True, dma=None):
        mykey = self.cur[eng]
        deps = {}
        for r in reads:
            t = r.w
            if t is not None:
                if t[0] == mykey and eng == "pe":
                    continue
                if deps.get(t[0], 0) < t[1]:
                    deps[t[0]] = t[1]
            if r.excl:
                for k, c in r.r.items():
                    if k != mykey and deps.get(k, 0) < c:
                        deps[k] = c
        same_ok = (eng != "pe")
        for w in writes:
            t = w.w
            if t is not None and (t[0] != mykey or same_ok):
                if deps.get(t[0], 0) < t[1]:
                    deps[t[0]] = t[1]
            for k, c in w.r.items():
                if (k != mykey or same_ok) and deps.get(k, 0) < c:
                    deps[k] = c
        e = self.engs[eng]
        wd = self.waited[eng]
        for k, c in deps.items():
            if wd.get(k, 0) < c:
                wd[k] = c
                sem, step = self.sems[k]
                e.wait_ge(sem, c * step)
        sigkey = dma if dma is not None else mykey
        if signal:
            self.counts[sigkey] += 1
            tok = (sigkey, self.counts[sigkey])
        else:
            tok = (sigkey, self.counts[sigkey] + 1)
        for r in reads:
            if r.r.get(tok[0], 0) < tok[1]:
                r.r[tok[0]] = tok[1]
        for w in writes:
            w.w = tok
            w.r = {}
        ins = fn(e)
        if signal:
            sem, step = self.sems[sigkey]
            ins.then_inc(sem, step)
        return tok

    def emit_dma(self, eng, fn, reads=(), writes=()):
        key = self.rr[self.rri % len(self.rr)]
        self.rri += 1
        if key in self.rrlast:
            self.wait_tokens(eng, [self.rrlast[key]])
        tok = self.emit(eng, fn, reads=reads, writes=writes, dma=key)
        self.rrlast[key] = tok
        return tok

    def wait_tokens(self, eng, toks):
        e = self.engs[eng]
        wd = self.waited[eng]
        for k, c in toks:
            if wd.get(k, 0) < c:
                wd[k] = c
                sem, step = self.sems[k]
                e.wait_ge(sem, c * step)


class Scope:
    def __init__(self, P):
        self.P = P
        self.es = ExitStack()
        self.rl = []

    def __enter__(self):
        self.es.__enter__()
        return self

    def __exit__(self, *a):
        for r in self.rl:
            self.P.free(r)
        return self.es.__exit__(*a)

    def sb(self, name, shape, dt):
        return self.es.enter_context(self.P.nc.sbuf_tensor(self.P.uniq(name), shape, dt))

    def R(self):
        r = self.P.res()
        self.rl.append(r)
        return r


def i_act(out, in_, func, bias=None, scale=1.0):
    if bias is None:
        return lambda e: e.activation(out=out, in_=in_, func=func, scale=scale)
    return lambda e: e.activation(out=out, in_=in_, func=func, bias=bias, scale=scale)


def i_tt(out, in0, in1, op):
    return lambda e: e.tensor_tensor(out=out, in0=in0, in1=in1, op=op)


def i_ts(out, in0, s1, s2, op0, op1=None):
    if op1 is None:
        return lambda e: e.tensor_scalar(out=out, in0=in0, scalar1=s1, scalar2=None, op0=op0)
    return lambda e: e.tensor_scalar(out=out, in0=in0, scalar1=s1, scalar2=s2, op0=op0, op1=op1)


def i_stt(out, in0, scalar, in1, op0, op1):
    return lambda e: e.scalar_tensor_tensor(out=out, in0=in0, scalar=scalar, in1=in1, op0=op0, op1=op1)


def i_copy(out, in_):
    return lambda e: e.tensor_copy(out=out, in_=in_)


def i_dma(out, in_, nc=None):
    return lambda e: e.dma_start(out=out, in_=in_)


def i_mm(out, lhsT, rhs, start, stop):
    return lambda e: e.matmul(out, lhsT=lhsT, rhs=rhs, start=start, stop=stop)


def i_tr(out, in_, ident):
    return lambda e: e.transpose(out=out, in_=in_, identity=ident)


def build_nc():
    nc = bass.Bass("TRN2", target_bir_lowering=False)

    def din(name, shape):
        return nc.dram_tensor(name, shape, F32, kind="ExternalInput").ap()

    def dout(name, shape):
        return nc.dram_tensor(name, shape, F32, kind="ExternalOutput").ap()

    xp = din("x_prompt", [SEQ, D])
    xs = din("x_sample", [SB, D])
    stc = din("state_conv", [NL, SB, 2, D])
    W = {name: din(name, shape) for name, shape in W_SHAPES}
    yp = dout("y_prompt", [SEQ, D])
    ys = dout("y_sample", [SB, D])
    ncp = dout("new_conv_prompt", [NL, 2, D])
    ncs = dout("new_conv_sample", [NL, SB, 2, D])
    ncv = dout("new_chunk_v", [NL, SB, D])

    with ExitStack() as es:
        P = Prog(nc, es)
        G = Scope(P)
        es.enter_context(G)
        es.enter_context(nc.allow_non_contiguous_dma(reason="small strided vectors"))

        h = G.sb("h", [128, 8, NTOK], F32)
        hR = [[G.R() for _ in TT] for _ in range(8)]
        n = G.sb("n", [128, 8, NTOK], BF16)
        nR = [[G.R() for _ in range(8)] for _ in TT]
        ring = G.sb("ring", [128, NSLOT, 1024], BF16)
        ident = G.sb("ident", [128, 128], F32)
        onesm = G.sb("onesm", [128, 128], F32)
        ones1 = G.sb("ones1", [128, 128], F32)
        epsb = G.sb("epsb", [128, 1], F32)
        mhalf = G.sb("mhalf", [128, 1], F32)
        cR = G.R()
        cv = G.sb("cv", [128, NL, 112], F32)
        hb = G.sb("hb", [128, NL, 16], F32)
        cvR = G.R()
        wTb = G.sb("wTb", [128, 4, 128], BF16)
        Rbc = G.sb("Rbc", [128, 4, 128], F32)
        bsbc = G.sb("bsbc", [128, 4, 128], F32)
        w00 = G.sb("w00", [SB, 4], F32)
        Dg = G.sb("Dg", [SB, 4, SB], BF16)
        S01 = G.sb("S01", [128, 2, 8, SB], F32)
        lsR = G.R()
        bsR = G.R()
        w0R = G.R()
        carry = G.sb("carry", [128, NL, 8, 2], F32)
        carR = [G.R() for _ in range(NL)]
        xgs = G.sb("xgs", [128, 8, SB], F32)
        xgsR = G.R()

        banks = [es.enter_context(nc.psum_tensor("pb%d" % i, [128, 512], F32)) for i in range(8)]
        bankR = [G.R() for _ in range(8)]
        for r_ in bankR:
            r_.excl = True
        bstate = [0]

        def next_bank():
            i = bstate[0]
            bstate[0] = (i + 1) % 8
            return banks[i], bankR[i]

        for key in ["xs0", "xs1", "xs2", "xs3", "xs4", "xs5", "os0", "os1", "os2"]:
            P.add_dma_sem(key)
        P.rr = ["rr%d" % i for i in range(20)]
        P.rri = 0
        P.rrlast = {}
        for key in P.rr:
            P.add_dma_sem(key)
        for s in range(NSLOT):
            P.add_dma_sem("w%d" % s)

        def mm_group(out_ap, bR, pairs, reads):
            nn = len(pairs)
            for i, (l, r) in enumerate(pairs):
                P.emit("pe", i_mm(out_ap, l, r, i == 0, i == nn - 1),
                       reads=reads if i == 0 else (), writes=[bR], signal=(i == nn - 1))

        specs = []

        def col(name, l, q):
            specs.append((W[name][l, :, q * 128:(q + 1) * 128].rearrange("(k p) c -> p k c", p=128), "col"))

        def row(name, l, r0, c0):
            specs.append((W[name][l, r0:r0 + 128, c0:c0 + 1024], "row"))

        def ffn_specs(l, pre):
            for (j0, j1) in GROUPS:
                for j in range(j0, j1):
                    col(pre + "_w_gate", l, j)
                    col(pre + "_w_up", l, j)
                for j in range(j0, j1):
                    row(pre + "_w_down", l, j * 128, 0)

        for l in range(NL):
            ffn_specs(l, "ffn1")
            for half in range(2):
                for k in range(8):
                    row("w_in", l, k * 128, 1024)
                for f in range(8):
                    for base in (0, 24, 32, 16, 40, 48):
                        col("w_in", l, base + f)
                for c in range(8):
                    col("w_out", l, c)
            ffn_specs(l, "ffn2")

        class WQ:
            def __init__(self):
                self.res = [G.R() for _ in range(NSLOT)]
                self.issued = 0
                self.got = 0
                self.released = [False] * len(specs)

            def pump(self):
                while self.issued < len(specs):
                    i = self.issued
                    if i >= NSLOT and not self.released[i - NSLOT]:
                        break
                    s = i % NSLOT
                    src, kind = specs[i]
                    if kind == "row":
                        dst = ring[:, s, :]
                    else:
                        dst = ring[:, s, :].rearrange("p (k c) -> p k c", k=8)
                    P.emit("pool", i_dma(dst, src), writes=[self.res[s]], dma="w%d" % s)
                    self.issued += 1

            def get(self):
                i = self.got
                self.got += 1
                self.pump()
                assert i < self.issued, "weight ring stalled"
                return i, i % NSLOT

            def release(self, i):
                self.released[i] = True
                self.pump()

        wq = WQ()

        def wcol(s):
            return ring[:, s, :].rearrange("p (k c) -> p k c", k=8)

        P.emit("pool", lambda e: e.memset(ones1[:], 1.0), writes=[cR])
        P.emit("pool", lambda e: e.memset(onesm[:], 1.0 / D), writes=[cR])
        P.emit("pool", lambda e: e.memset(epsb[:], EPS), writes=[cR])
        P.emit("pool", lambda e: e.memset(mhalf[:], -0.5), writes=[cR])
        P.emit("pool", lambda e: e.memset(carry[:], 0.0), writes=carR)
        P.emit("pool", lambda e: e.affine_select(out=ident[:], in_=ones1[:], pattern=[[-1, 128]],
                                                 compare_op=ALU.is_equal, fill=0.0, base=0,
                                                 channel_multiplier=1), reads=[cR], writes=[cR])

        class NormPipe:
            def __init__(self, sc, l, gcol, out_fn=None, final=None):
                self.final = final
                self.sq = sc.sb("sq", [128, 8, 512], F32)
                self.sqR = sc.R()
                self.ssq = [sc.sb("ssq", [128, 512], F32) for _ in range(3)]
                self.ssqR = [sc.R() for _ in range(3)]
                self.rs = [sc.sb("rs", [128, 512], F32) for _ in range(3)] + [sc.sb("rss", [128, SB], F32)]
                self.rsR = [sc.R() for _ in range(4)]
                self.l = l
                self.gcol = gcol
                self.k1 = 0
                self.k2 = 0
                self.slot = {}
                self.rslot = {}
                self.out_fn = out_fn

            def square(self, ti):
                t0, sz = TT[ti]
                P.emit("act", i_act(self.sq[:, :, 0:sz], h[:, :, t0:t0 + sz], AF.Square),
                       reads=[hR[c][ti] for c in range(8)], writes=[self.sqR])

            def square_h(self, ti, hf):
                t0, sz = TT[ti]
                P.emit("act", i_act(self.sq[:, hf * 4:hf * 4 + 4, 0:sz], h[:, hf * 4:hf * 4 + 4, t0:t0 + sz], AF.Square),
                       reads=[hR[c][ti] for c in range(hf * 4, hf * 4 + 4)], writes=[self.sqR])

            def adds_h(self, ti, hf):
                t0, sz = TT[ti]
                q = self.sq
                qR = self.sqR
                o = hf * 4
                P.emit("dve", i_tt(q[:, o:o + 2, 0:sz], q[:, o:o + 2, 0:sz], q[:, o + 2:o + 4, 0:sz], ALU.add), reads=[qR], writes=[qR])
                P.emit("dve", i_tt(q[:, o, 0:sz], q[:, o, 0:sz], q[:, o + 1, 0:sz], ALU.add), reads=[qR], writes=[qR])
                if hf == 1:
                    p = self.k1 % 3
                    self.k1 += 1
                    self.slot[ti] = p
                    P.emit("dve", i_tt(self.ssq[p][:, 0:sz], q[:, 0, 0:sz], q[:, 4, 0:sz], ALU.add), reads=[qR], writes=[self.ssqR[p]])

            def adds(self, ti):
                t0, sz = TT[ti]
                q = self.sq
                qR = self.sqR
                p = self.k1 % 3
                self.k1 += 1
                self.slot[ti] = p
                P.emit("dve", i_tt(q[:, 0:4, 0:sz], q[:, 0:4, 0:sz], q[:, 4:8, 0:sz], ALU.add), reads=[qR], writes=[qR])
                P.emit("dve", i_tt(q[:, 0:2, 0:sz], q[:, 0:2, 0:sz], q[:, 2:4, 0:sz], ALU.add), reads=[qR], writes=[qR])
                P.emit("dve", i_tt(self.ssq[p][:, 0:sz], q[:, 0, 0:sz], q[:, 1, 0:sz], ALU.add), reads=[qR], writes=[self.ssqR[p]])

            def mm(self, ti):
                t0, sz = TT[ti]
                p = self.slot[ti]
                if sz == 512:
                    r = self.k2 % 3
                    self.k2 += 1
                else:
                    r = 3
                self.rslot[ti] = r
                rs = self.rs[r]
                rR = self.rsR[r]
                if self.final is not None:
                    rs = self.final[0][:, t0:t0 + sz]
                    rR = self.final[1][ti]
                b, bR = next_bank()
                P.emit("pe", i_mm(b[:, 0:sz], onesm[:, :], self.ssq[p][:, 0:sz], True, True), reads=[self.ssqR[p], cR], writes=[bR])
                P.emit("act", i_act(rs[:, 0:sz], b[:, 0:sz], AF.Ln, bias=epsb[:, 0:1]), reads=[bR, cR], writes=[rR])
                P.emit("act", i_act(rs[:, 0:sz], rs[:, 0:sz], AF.Exp, scale=-0.5), reads=[rR], writes=[rR])

            def out(self, ti):
                if self.final is not None:
                    return
                t0, sz = TT[ti]
                r = self.rslot[ti]
                rs = self.rs[r]
                rR = self.rsR[r]
                if self.out_fn is not None:
                    self.out_fn(ti, rs, rR)
                    return
                for c in range(8):
                    self.scale_out(n[:, c, t0:t0 + sz], h[:, c, t0:t0 + sz], cv[:, self.l, self.gcol + c:self.gcol + c + 1],
                                   rs[:, 0:sz], sz, c, [hR[c][ti], rR, cvR], [nR[ti][c]])

            def scale_out(self, out, hin, g, rs_ap, sz, c, reads, writes):
                P.emit("dve", i_stt(out, hin, g, rs_ap, ALU.mult, ALU.mult), reads=reads, writes=writes)

        def run_tail(npipe, tiles, body, defer_last_out=False):
            K = len(tiles)
            st = {"outed": 0}
            bigs = [i for i, t in enumerate(tiles) if TT[t][1] == 512]
            last_big = bigs[-1] if bigs else -1
            for idx, ti in enumerate(tiles):
                big = TT[ti][1] == 512

                def mid(ti=ti, idx=idx, big=big):
                    npipe.square_h(ti, 0)
                    if big and idx >= 2 and not (defer_last_out and idx == last_big):
                        npipe.out(tiles[idx - 2])
                        st["outed"] = idx - 1
                    npipe.adds_h(ti, 0)

                if body(ti, mid):
                    npipe.square_h(ti, 1)
                    npipe.adds_h(ti, 1)
                else:
                    npipe.square(ti)
                    if big and idx >= 2:
                        npipe.out(tiles[idx - 2])
                        st["outed"] = idx - 1
                    npipe.adds(ti)
                if idx >= 1:
                    npipe.mm(tiles[idx - 1])
            npipe.mm(tiles[-1])
            for j in range(st["outed"], K):
                npipe.out(tiles[j])

        def load_cv(sc, defer):
            for l in range(NL):
                stg = sc.sb("cvstg", [112, 128], F32)
                sR = [sc.R() for _ in range(6)]
                srcs = [
                    (0, 56, W["b_in"][l].rearrange("(r c) -> r c", c=128)),
                    (56, 24, W["conv_w"][l].rearrange("k (r c) -> (k r) c", c=128)),
                    (80, 8, W["ffn1_norm"][l].rearrange("(r c) -> r c", c=128)),
                    (88, 8, W["mix_norm"][l].rearrange("(r c) -> r c", c=128)),
                    (96, 8, W["ffn2_norm"][l].rearrange("(r c) -> r c", c=128)),
                    (104, 8, W["final_norm"].rearrange("(r c) -> r c", c=128)),
                ]
                for di, (r0, nr, src) in enumerate(srcs):
                    P.emit_dma("act", i_dma(stg[r0:r0 + nr, :], src), writes=[sR[di]])

                def comp(l=l, stg=stg, sR=sR):
                    b, bR = next_bank()
                    P.emit("pe", i_tr(b[:, 0:112], stg[:, :], ident[0:112, 0:112]), reads=sR + [cR], writes=[bR])
                    P.emit("dve", i_copy(cv[:, l, :], b[:, 0:112]), reads=[bR], writes=[cvR])
                    P.emit("dve", i_ts(hb[:, l, :], cv[:, l, 40:56], 0.5, None, ALU.mult), reads=[cvR], writes=[cvR])
                defer.append(comp)

        def prologue_x(sc, hook):
            if True:
                NXS = 6
                xst = [sc.sb("xst", [128, 1024], F32) for _ in range(NXS)]
                xR = [sc.R() for _ in range(NXS)]
                npipe = NormPipe(sc, 0, 80)
                st_ = {"ev": 0}

                def xbody(tt, mid=None):
                    if tt < 4:
                        for i in range(tt * 4, tt * 4 + 4):
                            s = i % NXS
                            tok = P.emit("sp", i_dma(xst[s][:, :], xp[i * 128:(i + 1) * 128, :]), writes=[xR[s]], dma="xs%d" % s)
                            if i == 9:
                                hook(0)
                            if i == 11:
                                P.wait_tokens("pool", [tok])
                                wq.pump()
                            for hf in range(2):
                                b, bR = next_bank()
                                for q in range(4):
                                    c = hf * 4 + q
                                    P.emit("pe", i_tr(b[:, q * 128:(q + 1) * 128], xst[s][:, c * 128:(c + 1) * 128], ident[:, :]),
                                           reads=[xR[s], cR], writes=[bR], signal=(q == 3))
                                dst = h[:, hf * 4:(hf + 1) * 4, i * 128:(i + 1) * 128]
                                src = b[:, :].rearrange("p (q t) -> p q t", q=4)
                                wr = [hR[hf * 4 + q][i // 4] for q in range(4)]
                                if st_["ev"] % 2 == 0 or i < 6:
                                    P.emit("dve", i_copy(dst, src), reads=[bR], writes=wr)
                                else:
                                    P.emit("act", i_act(dst, src, AF.Copy), reads=[bR], writes=wr)
                                st_["ev"] += 1
                    else:
                        hook(1)
                        s = 16 % NXS
                        P.emit("sp", i_dma(xst[s][0:SB, :], xs[:, :]), writes=[xR[s]], dma="xs%d" % s)
                        b, bR = next_bank()
                        for c in range(8):
                            P.emit("pe", i_tr(b[:, c * SB:(c + 1) * SB], xst[s][0:SB, c * 128:(c + 1) * 128], ident[0:SB, 0:SB]),
                                   reads=[xR[s], cR], writes=[bR], signal=(c == 7))
                        P.emit("dve", i_copy(h[:, :, SEQ:NTOK], b[:, 0:8 * SB].rearrange("p (c t) -> p c t", c=8)),
                               reads=[bR], writes=[hR[c][4] for c in range(8)])

                run_tail(npipe, [0, 1, 2, 3, 4], xbody)

        def ffn(l, pre, tail):
            with Scope(P) as sc:
                act = sc.sb("act", [128, 8, NTOK], BF16)
                actR = [[sc.R() for _ in TT] for _ in range(8)]
                stmp = [sc.sb("stmp", [128, 512], F32) for _ in range(2)]
                stR = [sc.R() for _ in range(2)]
                if tail == "final":
                    npipe = NormPipe(sc, 0, 104, final=(rs_all, rsF))
                else:
                    npipe = NormPipe(sc, tail[0], tail[1]) if tail is not None else None
                ui = 0
                for gi, (j0, j1) in enumerate(GROUPS):
                    for j in range(j0, j1):
                        ig, sg = wq.get()
                        iu, su = wq.get()
                        Wg = wcol(sg)
                        Wu = wcol(su)
                        for ti, (t0, sz) in enumerate(TT):
                            ba, bRa = next_bank()
                            mm_group(ba[:, 0:sz], bRa, [(Wg[:, k, :], n[:, k, t0:t0 + sz]) for k in range(8)],
                                     [wq.res[sg]] + nR[ti])
                            bb, bRb = next_bank()
                            mm_group(bb[:, 0:sz], bRb, [(Wu[:, k, :], n[:, k, t0:t0 + sz]) for k in range(8)],
                                     [wq.res[su]] + nR[ti])
                            st = stmp[ui % 2]
                            sR = stR[ui % 2]
                            P.emit("act", i_act(st[:, 0:sz], ba[:, 0:sz], AF.Silu), reads=[bRa], writes=[sR])
                            P.emit("dve", i_tt(act[:, j - j0, t0:t0 + sz], bb[:, 0:sz], st[:, 0:sz], ALU.mult),
                                   reads=[bRb, sR], writes=[actR[j - j0][ti]])
                            ui += 1
                        wq.release(ig)
                        wq.release(iu)
                    ds = [wq.get() for _ in range(j0, j1)]
                    last = (gi == len(GROUPS) - 1) and npipe is not None

                    def dbody(ti, mid=None, ds=ds):
                        t0, sz = TT[ti]
                        for c in range(8):
                            if c == 4 and mid is not None and sz == 512:
                                mid()
                            b, bR = next_bank()
                            mm_group(b[:, 0:sz], bR,
                                     [(ring[:, s, c * 128:(c + 1) * 128], act[:, jl, t0:t0 + sz]) for jl, (_, s) in enumerate(ds)],
                                     [wq.res[s] for (_, s) in ds] + [actR[jl][ti] for jl in range(len(ds))])
                            P.emit("dve", i_stt(h[:, c, t0:t0 + sz], b[:, 0:sz], 0.5, h[:, c, t0:t0 + sz], ALU.mult, ALU.add),
                                   reads=[bR, hR[c][ti]], writes=[hR[c][ti]])
                        return mid is not None and sz == 512

                    if last:
                        run_tail(npipe, list(range(len(TT))), dbody, defer_last_out=(tail != "final" and tail[1] == 88))
                    else:
                        for ti in range(len(TT)):
                            dbody(ti)
                    for (i, _) in ds:
                        wq.release(i)

        def layer_setup(l, sc, q, defer):
            if True:
                wst = sc.sb("wst", [128, 4, 128], F32)
                wT32 = sc.sb("wT32", [128, 4, 128], F32)
                sts = sc.sb("sts", [SB, 2, D], F32)
                tR = sc.R()
                t2R = sc.R()
                sR = sc.R()
                P.emit_dma(q, i_dma(wst[:, :, :], W["w_spatial"][l].rearrange("g t s -> t g s")), writes=[tR])
                P.emit_dma(q, i_dma(bsbc[:, :, :].rearrange("p g t -> p (g t)"),
                                   W["b_spatial"][l].rearrange("g t -> (g t)").partition_broadcast(128)),
                       writes=[bsR])
                P.emit_dma(q, i_dma(w00[:, :], W["w_spatial"][l, :, 0, 0].partition_broadcast(SB)), writes=[w0R])
                P.emit_dma(q, i_dma(sts[:, :, :], stc[l]), writes=[sR])
                P.emit_dma(q, i_dma(ncs[l, :, 0, :], stc[l, :, 1, :]))
                defer.append(lambda: ls_compute(l, wst, wT32, sts, tR, t2R, sR))

        def ls_compute(l, wst, wT32, sts, tR, t2R, sR):
            if True:
                P.emit("pool", lambda e: e.affine_select(out=wst[:, :, :], in_=wst[:, :, :], pattern=[[0, 4], [-1, 128]],
                                                         compare_op=ALU.is_ge, fill=0.0, base=0, channel_multiplier=1),
                       reads=[tR], writes=[tR])
                b, bR = next_bank()
                for g in range(4):
                    P.emit("pe", i_tr(b[:, g * 128:(g + 1) * 128], wst[:, g, :], ident[:, :]), reads=[tR, cR], writes=[bR],
                           signal=(g == 3))
                P.emit("dve", i_copy(wT32[:, :, :], b[:, :].rearrange("p (g t) -> p g t", g=4)), reads=[bR], writes=[t2R])
                P.emit("act", i_act(wTb[:, :, :], wT32[:, :, :], AF.Copy), reads=[t2R], writes=[lsR])
                b2, b2R = next_bank()
                P.emit("pe", i_mm(b2[:, :], ones1[:, :], wT32[:, :, :].rearrange("p g t -> p (g t)"), True, True),
                       reads=[t2R, cR], writes=[b2R])
                P.emit("dve", i_copy(Rbc[:, :, :].rearrange("p g t -> p (g t)"), b2[:, :]), reads=[b2R], writes=[lsR])
                for g in range(4):
                    P.emit("dve", i_ts(Dg[:, g, :], ident[0:SB, 0:SB], w00[:, g:g + 1], None, ALU.mult),
                           reads=[lsR, cR, w0R], writes=[lsR])
                b3, b3R = next_bank()
                for k in range(2):
                    for c in range(8):
                        P.emit("pe", i_tr(b3[:, (k * 8 + c) * SB:(k * 8 + c + 1) * SB], sts[:, k, c * 128:(c + 1) * 128],
                                          ident[0:SB, 0:SB]), reads=[sR, cR], writes=[b3R], signal=(k == 1 and c == 7))
                P.emit("dve", i_copy(S01[:, :, :, :].rearrange("p k c t -> p (k c t)"), b3[:, 0:256]), reads=[b3R], writes=[lsR])

        def mixer(l, tail):
            for half in range(2):
                tis = [0, 1] if half == 0 else [2, 3, 4]
                mcol = {0: 0, 1: 512, 2: 0, 3: 512, 4: 1024}
                with Scope(P) as sc:
                    vln = sc.sb("vln", [128, 9, 1024], BF16)
                    vlnR = [sc.R() for _ in range(9)]
                    m = sc.sb("m", [128, 8, 1040], BF16)
                    mR = {ti: [sc.R() for _ in range(8)] for ti in tis}
                    with Scope(P) as sa:
                        NVB = 3
                        vb = [sa.sb("vb", [128, 1024], F32) for _ in range(NVB)]
                        vbR = [sa.R() for _ in range(NVB)]
                        bvbc = sa.sb("bvbc", [128, 1024], F32)
                        bvR = sa.R()
                        st12 = sa.sb("st12", [128, NVB, 12], F32)
                        junk = sa.sb("junk", [128, 1024], BF16)
                        jR = sa.R()
                        mv = sa.sb("mv", [128, NVB, 4], F32)
                        smR = [sa.R() for _ in range(NVB)]
                        P.emit_dma("sp", i_dma(bvbc[:, :], W["b_in"][l, 1024:2048].partition_broadcast(128)), writes=[bvR])
                        if half == 1:
                            gb16 = sa.sb("gb16", [SB, 2, 1024], F32)
                            gbR = sa.R()
                            P.emit_dma("sp", i_dma(gb16[:, 0, :], W["v_ln_gain"][l].partition_broadcast(SB)), writes=[gbR])
                            P.emit_dma("sp", i_dma(gb16[:, 1, :], W["v_ln_bias"][l].partition_broadcast(SB)), writes=[gbR])
                        wv = [wq.get() for _ in range(8)]
                        tiles = [(half * 1024 + i * 128, 128, i) for i in range(8)]
                        if half == 1:
                            tiles.insert(0, (SEQ, SB, 8))
                        def stage_a(ix):
                            t0, sz, vi = tiles[ix]
                            ti = 4 if sz == SB else t0 // 512
                            p = ix % NVB
                            v_ = vb[p]
                            vR = vbR[p]
                            for hh in range(2):
                                b, bR = next_bank()
                                mm_group(b[0:sz, :], bR,
                                         [(n[:, k, t0:t0 + sz], ring[:, wv[k][1], hh * 512:(hh + 1) * 512]) for k in range(8)],
                                         [wq.res[s] for (_, s) in wv] + nR[ti])
                                P.emit("dve", i_tt(v_[0:sz, hh * 512:(hh + 1) * 512], b[0:sz, :], bvbc[0:sz, hh * 512:(hh + 1) * 512], ALU.add),
                                       reads=[bR, bvR], writes=[vR])
                            P.emit("act", lambda e, o=v_[0:sz, :], a=st12[0:sz, p, 0:1]: e.activation(out=o, in_=o, func=AF.Gelu, accum_out=a),
                                   reads=[vR], writes=[vR, smR[p]])
                            P.emit("act", lambda e, o=junk[0:sz, :], i=v_[0:sz, :], a=st12[0:sz, p, 1:2]:
                                   e.activation(out=o, in_=i, func=AF.Square, accum_out=a),
                                   reads=[vR], writes=[jR, smR[p]])

                        def stage_b(ix):
                            t0, sz, vi = tiles[ix]
                            p = ix % NVB
                            v_ = vb[p]
                            vR = vbR[p]
                            P.emit("dve", i_ts(mv[0:sz, p, 0:1], st12[0:sz, p, 0:1], 1.0 / D, None, ALU.mult), reads=[smR[p]], writes=[smR[p]])
                            P.emit("dve", i_tt(mv[0:sz, p, 1:2], mv[0:sz, p, 0:1], mv[0:sz, p, 0:1], ALU.mult), reads=[smR[p]], writes=[smR[p]])
                            P.emit("dve", i_stt(mv[0:sz, p, 3:4], st12[0:sz, p, 1:2], 1.0 / D, mv[0:sz, p, 1:2], ALU.mult, ALU.subtract),
                                   reads=[smR[p]], writes=[smR[p]])
                            P.emit("dve", i_ts(mv[0:sz, p, 3:4], mv[0:sz, p, 3:4], EPS, None, ALU.add), reads=[smR[p]], writes=[smR[p]])
                            P.emit("pool", i_tt(mv[0:sz, p, 2:3], mv[0:sz, p, 3:4], mhalf[0:sz, 0:1], ALU.pow),
                                   reads=[smR[p], cR], writes=[smR[p]])

                        def stage_c(ix):
                            t0, sz, vi = tiles[ix]
                            p = ix % NVB
                            v_ = vb[p]
                            vR = vbR[p]
                            if sz == 128:
                                P.emit("dve", i_ts(vln[:, vi, :], v_[:, :], mv[:, p, 0:1], mv[:, p, 2:3], ALU.subtract, ALU.mult),
                                       reads=[vR, smR[p]], writes=[vlnR[vi]])
                            else:
                                P.emit("dve", i_ts(v_[0:sz, :], v_[0:sz, :], mv[0:sz, p, 0:1], mv[0:sz, p, 2:3], ALU.subtract, ALU.mult),
                                       reads=[vR, smR[p]], writes=[vR])
                                P.emit("act", i_act(vln[0:sz, vi, :], v_[0:sz, :], AF.Copy), reads=[vR], writes=[vlnR[vi]])
                                P.emit("dve", i_tt(v_[0:sz, :], v_[0:sz, :], gb16[:, 0, :], ALU.mult), reads=[vR, gbR], writes=[vR])
                                P.emit("dve", i_tt(v_[0:sz, :], v_[0:sz, :], gb16[:, 1, :], ALU.add), reads=[vR, gbR], writes=[vR])
                                P.emit_dma("sp", i_dma(ncv[l], v_[0:sz, :]), reads=[vR])

                        NT_ = len(tiles)
                        for ix in range(NT_):
                            stage_a(ix)
                            if ix >= 1:
                                stage_b(ix - 1)
                            if ix >= 2:
                                stage_c(ix - 2)
                        stage_b(NT_ - 1)
                        stage_c(NT_ - 2)
                        stage_c(NT_ - 1)
                        for (i, _) in wv:
                            wq.release(i)
                    with Scope(P) as sbx:
                        NTMP = 2
                        tmp = {nm: [sbx.sb(nm, [128, 512], F32) for _ in range(NTMP if nm in ("u", "c", "ta", "tb") else 1)]
                               for nm in ("u", "c", "ta", "tb", "ya", "yb", "cv1", "zz")}
                        tmpR = {nm: [sbx.R() for _ in tmp[nm]] for nm in tmp}
                        xg = sbx.sb("xg", [128, 1026], F32)
                        xgR = sbx.R()
                        Ef = sbx.sb("Ef", [128, 128], F32)
                        EfR = sbx.R()
                        ui = 0
                        for f in range(8):
                            g = f // 2
                            ws = [wq.get() for _ in range(6)]
                            (iu_, su_), (ic_, sc_), (ix_, sx_), (ib_, sb__), (iga, sga), (igb, sgb) = ws
                            gain_f = cv
                            P.emit("dve", i_stt(Ef[:, :], Rbc[:, g, :], lnb[:, l, f:f + 1], bsbc[:, g, :], ALU.mult, ALU.add),
                                   reads=[lsR, lnR, bsR], writes=[EfR])
                            P.emit("dve", i_copy(xg[:, 0:2], carry[:, l, f, :]), reads=[carR[l]], writes=[xgR])
                            for ti in tis:
                                t0, sz = TT[ti]
                                mc = mcol[ti]
                                p = ui % NTMP
                                ui += 1
                                T = {nm: tmp[nm][p % len(tmp[nm])] for nm in tmp}
                                TR = {nm: tmpR[nm][p % len(tmp[nm])] for nm in tmp}
                                nrd = nR[ti]

                                def lin(slot):
                                    b, bR = next_bank()
                                    Wc = wcol(slot)
                                    mm_group(b[:, 0:sz], bR, [(Wc[:, k, :], n[:, k, t0:t0 + sz]) for k in range(8)],
                                             [wq.res[slot]] + nrd)
                                    return b, bR

                                b, bR = lin(su_)
                                P.emit("act", i_act(T["u"][:, 0:sz], b[:, 0:sz], AF.Gelu, bias=cv[:, l, f:f + 1]),
                                       reads=[bR, cvR], writes=[TR["u"]])
                                b, bR = next_bank()
                                if sz == 512:
                                    for c4 in range(4):
                                        vi = (t0 - half * 1024) // 128 + c4
                                        P.emit("pe", i_mm(b[:, c4 * 128:(c4 + 1) * 128], vln[:, vi, f * 128:(f + 1) * 128], wTb[:, g, :], True, True),
                                               reads=[vlnR[vi], lsR], writes=[bR], signal=(c4 == 3))
                                    P.emit("dve", i_stt(T["zz"][:, :].rearrange("p (a t) -> p a t", a=4),
                                                        b[:, :].rearrange("p (a t) -> p a t", a=4), lng[:, l, f:f + 1],
                                                        Ef[:, :].unsqueeze(1).to_broadcast([128, 4, 128]), ALU.mult, ALU.add),
                                           reads=[bR, EfR, lnR], writes=[TR["zz"]])
                                else:
                                    P.emit("pe", i_mm(b[:, 0:sz], vln[0:SB, 8, f * 128:(f + 1) * 128], Dg[:, g, :], True, True),
                                           reads=[vlnR[8], lsR], writes=[bR])
                                    P.emit("dve", i_stt(T["zz"][:, 0:sz], b[:, 0:sz], lng[:, l, f:f + 1],
                                                        Ef[:, 0:1].to_broadcast([128, sz]), ALU.mult, ALU.add),
                                           reads=[bR, EfR, lnR], writes=[TR["zz"]])
                                P.emit("dve", i_tt(T["ya"][:, 0:sz], T["zz"][:, 0:sz], T["u"][:, 0:sz], ALU.mult),
                                       reads=[TR["zz"], TR["u"]], writes=[TR["ya"]])
                                b, bR = lin(sc_)
                                P.emit("act", i_act(T["c"][:, 0:sz], b[:, 0:sz], AF.Identity, bias=cv[:, l, 24 + f:25 + f]),
                                       reads=[bR, cvR], writes=[TR["c"]])
                                b, bR = lin(sx_)
                                if sz == 512:
                                    xo = 2 + mc
                                    P.emit("dve", i_stt(xg[:, xo:xo + sz], b[:, 0:sz], cv[:, l, 32 + f:33 + f], T["c"][:, 0:sz], ALU.add, ALU.mult),
                                           reads=[bR, TR["c"], cvR], writes=[xgR])
                                    x0 = xg[:, mc:mc + sz]
                                    x1 = xg[:, mc + 1:mc + 1 + sz]
                                    x2 = xg[:, mc + 2:mc + 2 + sz]
                                    xrd = [xgR]
                                else:
                                    P.emit("dve", i_stt(xgs[:, f, :], b[:, 0:sz], cv[:, l, 32 + f:33 + f], T["c"][:, 0:sz], ALU.add, ALU.mult),
                                           reads=[bR, TR["c"], cvR], writes=[xgsR])
                                    x0 = S01[:, 0, f, :]
                                    x1 = S01[:, 1, f, :]
                                    x2 = xgs[:, f, :]
                                    xrd = [xgsR, lsR]
                                cvt = T["cv1"]
                                P.emit("dve", i_ts(cvt[:, 0:sz], x0, cv[:, l, 56 + f:57 + f], None, ALU.mult),
                                       reads=xrd + [cvR], writes=[TR["cv1"]])
                                P.emit("dve", i_stt(cvt[:, 0:sz], x1, cv[:, l, 64 + f:65 + f], cvt[:, 0:sz], ALU.mult, ALU.add),
                                       reads=xrd + [cvR, TR["cv1"]], writes=[TR["cv1"]])
                                P.emit("dve", i_stt(cvt[:, 0:sz], x2, cv[:, l, 72 + f:73 + f], cvt[:, 0:sz], ALU.mult, ALU.add),
                                       reads=xrd + [cvR, TR["cv1"]], writes=[TR["cv1"]])
                                b, bR = lin(sb__)
                                P.emit("dve", i_stt(T["yb"][:, 0:sz], b[:, 0:sz], cv[:, l, 16 + f:17 + f], cvt[:, 0:sz], ALU.add, ALU.mult),
                                       reads=[bR, TR["cv1"], cvR], writes=[TR["yb"]])
                                b, bR = lin(sga)
                                P.emit("act", i_act(T["ta"][:, 0:sz], b[:, 0:sz], AF.Tanh, bias=hb[:, l, f:f + 1], scale=0.5),
                                       reads=[bR, cvR], writes=[TR["ta"]])
                                b, bR = lin(sgb)
                                P.emit("act", i_act(T["tb"][:, 0:sz], b[:, 0:sz], AF.Tanh, bias=hb[:, l, 8 + f:9 + f], scale=0.5),
                                       reads=[bR, cvR], writes=[TR["tb"]])
                                P.emit("dve", i_stt(T["ya"][:, 0:sz], T["ta"][:, 0:sz], 1.0, T["ya"][:, 0:sz], ALU.add, ALU.mult),
                                       reads=[TR["ta"], TR["ya"]], writes=[TR["ya"]])
                                P.emit("dve", i_stt(T["yb"][:, 0:sz], T["tb"][:, 0:sz], 1.0, T["yb"][:, 0:sz], ALU.add, ALU.mult),
                                       reads=[TR["tb"], TR["yb"]], writes=[TR["yb"]])
                                P.emit("dve", i_tt(m[:, f, mc:mc + sz], T["ya"][:, 0:sz], T["yb"][:, 0:sz], ALU.add),
                                       reads=[TR["ya"], TR["yb"]], writes=[mR[ti][f]])
                            P.emit("dve", i_copy(carry[:, l, f, :], xg[:, 1024:1026]), reads=[xgR], writes=[carR[l]])
                            for (i, _) in ws:
                                wq.release(i)
                    if half == 1:
                        with Scope(P) as so_:
                            o16 = so_.sb("o16", [SB, D], F32)
                            oR = so_.R()
                            b0, b0R = next_bank()
                            b1, b1R = next_bank()
                            for c in range(8):
                                bb_, bbR = (b0, b0R) if c < 4 else (b1, b1R)
                                P.emit("pe", i_tr(bb_[0:SB, (c % 4) * 128:(c % 4 + 1) * 128], xgs[:, c, :], ident[:, :]),
                                       reads=[xgsR, cR], writes=[bbR], signal=(c % 4 == 3))
                            P.emit("dve", i_copy(o16[:, 0:512], b0[0:SB, :]), reads=[b0R], writes=[oR])
                            P.emit("dve", i_copy(o16[:, 512:1024], b1[0:SB, :]), reads=[b1R], writes=[oR])
                            tok = P.emit_dma("sp", i_dma(ncs[l, :, 1, :], o16[:, :]), reads=[oR])
                            P.wait_tokens("sp", [tok])
                    with Scope(P) as scC:
                        npipe = NormPipe(scC, tail[0], tail[1])
                        wo = [wq.get() for _ in range(8)]
                        def cbody(ti, mid=None):
                            t0, sz = TT[ti]
                            mc = mcol[ti]
                            for c in range(8):
                                if c == 4 and mid is not None and sz == 512:
                                    mid()
                                io, so = wo[c]
                                Wc = wcol(so)
                                b, bR = next_bank()
                                mm_group(b[:, 0:sz], bR, [(Wc[:, k, :], m[:, k, mc:mc + sz]) for k in range(8)],
                                         [wq.res[so]] + mR[ti])
                                P.emit("dve", i_stt(h[:, c, t0:t0 + sz], b[:, 0:sz], 0.5, h[:, c, t0:t0 + sz], ALU.mult, ALU.add),
                                       reads=[bR, hR[c][ti]], writes=[hR[c][ti]])
                            return mid is not None and sz == 512

                        run_tail(npipe, tis, cbody)
                        for (io, _) in wo:
                            wq.release(io)
            for k in range(2):
                P.emit_dma("sp", i_dma(ncp[l, k, :].rearrange("(f p) -> p f", p=128), carry[:, l, :, k]), reads=[carR[l]])

        lng = G.sb("lng", [128, NL, 8], F32)
        lnb = G.sb("lnb", [128, NL, 8], F32)
        lnR = G.R()
        def load_ln(sc, defer):
            stg = sc.sb("lnstg", [32, 128], F32)
            sR = sc.R()
            sR2 = sc.R()
            P.emit_dma("act", i_dma(stg[0:16, :], W["v_ln_gain"].rearrange("l (r c) -> (l r) c", c=128)), writes=[sR])
            P.emit_dma("act", i_dma(stg[16:32, :], W["v_ln_bias"].rearrange("l (r c) -> (l r) c", c=128)), writes=[sR2])

            def comp():
                b, bR = next_bank()
                P.emit("pe", i_tr(b[:, 0:32], stg[:, :], ident[0:32, 0:32]), reads=[sR, sR2, cR], writes=[bR])
                P.emit("dve", i_copy(lng[:, :, :].rearrange("p l c -> p (l c)"), b[:, 0:16]), reads=[bR], writes=[lnR])
                P.emit("dve", i_copy(lnb[:, :, :].rearrange("p l c -> p (l c)"), b[:, 16:32]), reads=[bR], writes=[lnR])
            defer.append(comp)

        for l in range(NL):
            P.new_epoch()
            if l == 0:
                with Scope(P) as s0:
                    dfr = []
                    dfr2 = []
                    load_cv(s0, dfr)
                    load_ln(s0, dfr2)
                    layer_setup(0, s0, "act", dfr2)
                    prologue_x(s0, lambda k: [f() for f in (dfr if k == 0 else dfr2)])
            ffn(l, "ffn1", (l, 88))
            P.new_epoch()
            mixer(l, (l, 96))
            P.new_epoch()
            if l + 1 < NL:
                with Scope(P) as s1:
                    dfr = []
                    layer_setup(l + 1, s1, "sp", dfr)
                    for f in dfr:
                        f()
            if l + 1 < NL:
                ffn(l, "ffn2", (l + 1, 80))
            else:
                rs_all = n[:, 0:2, :].rearrange("p c t -> p (c t)").bitcast(F32)
                rsF = [P.alias([nR[ti][k] for ti in range(len(TT)) for k in range(8)]) for _ in TT]
                ffn(l, "ffn2", "final")

        P.new_epoch()
        with Scope(P) as sc:
            yT = [sc.sb("yT", [128, 8, 512], F32) for _ in range(2)]
            yR = [sc.R() for _ in range(2)]
            NOS = 3
            ost = [sc.sb("ost", [128, 1024], F32) for _ in range(NOS)]
            oR = [sc.R() for _ in range(NOS)]
            oc = 0
            for ti, (t0, sz) in enumerate(TT):
                y = yT[ti % 2]
                yr = yR[ti % 2]
                for c in range(8):
                    P.emit("dve", i_stt(y[:, c, 0:sz], h[:, c, t0:t0 + sz], cv[:, 0, 104 + c:105 + c], rs_all[:, t0:t0 + sz], ALU.mult, ALU.mult),
                           reads=[hR[c][ti], rsF[ti], cvR], writes=[yr])
                nb = 4 if sz == 512 else 1
                bw = 128 if sz == 512 else SB
                for bk in range(nb):
                    s_ = oc % NOS
                    oc += 1
                    for hf in range(2):
                        b, bR = next_bank()
                        for q in range(4):
                            c = hf * 4 + q
                            P.emit("pe", i_tr(b[0:bw, q * 128:(q + 1) * 128], y[:, c, bk * 128:bk * 128 + bw], ident[:, :]),
                                   reads=[yr, cR], writes=[bR], signal=(q == 3))
                        P.emit("act", i_act(ost[s_][0:bw, hf * 512:(hf + 1) * 512], b[0:bw, :], AF.Copy), reads=[bR], writes=[oR[s_]])
                    if sz == 512:
                        dst = yp[t0 + bk * 128:t0 + (bk + 1) * 128, :]
                    else:
                        dst = ys[:, :]
                    P.emit("sp", i_dma(dst, ost[s_][0:bw, :]), reads=[oR[s_]], dma="os%d" % s_)
            P.wait_tokens("sp", [(k, P.counts[k]) for k in P.rr + ["os0", "os1", "os2"]])
    return nc


_NC = None


def kernel(**inputs):
    global _NC
    if _NC is None:
        _NC = build_nc()
    nc = _NC
    f = lambda a: np.ascontiguousarray(np.asarray(a, dtype=np.float32))
    xp = f(inputs["x_prompt"])
    xs = f(inputs["x_sample"])
    st = f(inputs["state_conv"])
    shared = {name: f(inputs[name]) for name, _ in W_SHAPES}
    in_maps = []
    for c in range(NCORE):
        d = dict(shared)
        d["x_prompt"] = np.ascontiguousarray(xp[c])
        d["x_sample"] = np.ascontiguousarray(xs[c * SB:(c + 1) * SB, 0, :])
        d["state_conv"] = np.ascontiguousarray(st[:, c * SB:(c + 1) * SB])
        in_maps.append(d)
    res = run_bass_kernel_spmd(nc, in_maps, core_ids=list(range(NCORE)))
    R = res.results
    y_prompt = np.stack([R[c]["y_prompt"] for c in range(NCORE)], axis=0)
    y_sample = np.concatenate([R[c]["y_sample"] for c in range(NCORE)], axis=0)[:, None, :]
    ncp = np.stack([R[c]["new_conv_prompt"] for c in range(NCORE)], axis=1)
    ncs = np.concatenate([R[c]["new_conv_sample"] for c in range(NCORE)], axis=1)
    ncv = np.concatenate([R[c]["new_chunk_v"] for c in range(NCORE)], axis=1)[:, :, None, :]
    return (y_prompt.astype(np.float32), y_sample.astype(np.float32), ncp.astype(np.float32),
            ncs.astype(np.float32), ncv.astype(np.float32))
```
